# Optimizing a Trainium2 kernel written in Bass

```python
import jax, jax.numpy as jnp
from jax import lax
import numpy as np

D_MODEL = 1024
BATCH = 32
SEQ = 2048
DEPTH = 1
DEC_BATCH = 2
DEC_SEQ = 8192
PAST_LEN = 128

GRID_W = 64
HEAD_DIM = 64
N_HEADS_A = 8
N_KV_HEADS_A = 2
N_HEADS_B = 8
WIDTH_A = N_HEADS_A * HEAD_DIM
WIDTH_KV_A = N_KV_HEADS_A * HEAD_DIM
WIDTH_B = N_HEADS_B * HEAD_DIM
NA_MAX_ROWS = 8
NA_COLS = 16
Q_BLOCK = 128
ROPE_THETA = 10000.0
D_FF = 2816
CONV_WIDTH = 3
EPS = 1e-6
SPLIT_SIZES = [WIDTH_A, WIDTH_KV_A, WIDTH_KV_A, WIDTH_B, WIDTH_B, WIDTH_B, D_MODEL, D_MODEL]
D_IN = sum(SPLIT_SIZES)
SPLIT_POINTS = [int(v) for v in np.cumsum(SPLIT_SIZES)[:-1]]

kernel_name = "hybrid_gqa_natten_gated_encoder"


def rms_norm(x, w):
    xf = x.astype(jnp.float32)
    y = xf * lax.rsqrt(jnp.mean(xf * xf, axis=-1, keepdims=True) + EPS)
    return (y * w.astype(jnp.float32)).astype(x.dtype)


def axial_rope_tables(seq_len, dtype):
    t = np.arange(seq_len)
    row = (t // GRID_W).astype(np.float32)
    col = (t % GRID_W).astype(np.float32)
    half = HEAD_DIM // 2
    freqs = (ROPE_THETA ** (-np.arange(0, half, 2, dtype=np.float32) / half)).astype(np.float32)
    ang_r = row[:, None] * freqs[None, :]
    ang_c = col[:, None] * freqs[None, :]
    ang = np.concatenate([ang_r, ang_r, ang_c, ang_c], axis=-1).astype(np.float32)
    return jnp.asarray(np.cos(ang), dtype), jnp.asarray(np.sin(ang), dtype)


def rotate_half(x):
    h = x.shape[-1] // 2
    return jnp.concatenate([-x[..., h:], x[..., :h]], axis=-1)


def apply_axial_rope(x, cos, sin):
    half = HEAD_DIM // 2
    rot = jnp.concatenate([rotate_half(x[..., :half]), rotate_half(x[..., half:])], axis=-1)
    return x * cos[:, None, :] + rot * sin[:, None, :]


def gqa_branch(q, k, v, q_norm_w, k_norm_w):
    B, S = q.shape[:2]
    G = N_HEADS_A // N_KV_HEADS_A
    q = rms_norm(q.reshape(B, S, N_HEADS_A, HEAD_DIM), q_norm_w)
    k = rms_norm(k.reshape(B, S, N_KV_HEADS_A, HEAD_DIM), k_norm_w)
    v = v.reshape(B, S, N_KV_HEADS_A, HEAD_DIM)
    cos, sin = axial_rope_tables(S, q.dtype)
    q = apply_axial_rope(q, cos, sin)
    k = apply_axial_rope(k, cos, sin)
    n_blocks = S // Q_BLOCK
    qb = q.reshape(B, n_blocks, Q_BLOCK, N_KV_HEADS_A, G, HEAD_DIM).transpose(1, 0, 2, 3, 4, 5)
    scale = HEAD_DIM ** -0.5

    def attend_block(q_blk):
        s = jnp.einsum('bqkgd,bskd->bkgqs', q_blk, k).astype(jnp.float32) * scale
        p = jax.nn.softmax(s, axis=-1).astype(v.dtype)
        return jnp.einsum('bkgqs,bskd->bqkgd', p, v)

    o = lax.map(attend_block, qb)
    return o.transpose(1, 0, 2, 3, 4, 5).reshape(B, S, WIDTH_A)


def neighbourhood_branch(q, k, v, rpb):
    B, S = q.shape[:2]
    rows = S // GRID_W
    wr = min(NA_MAX_ROWS, rows)
    qg = q.reshape(B, rows, GRID_W, N_HEADS_B, HEAD_DIM).transpose(1, 0, 2, 3, 4)
    kg = k.reshape(B, rows, GRID_W, N_HEADS_B, HEAD_DIM)
    vg = v.reshape(B, rows, GRID_W, N_HEADS_B, HEAD_DIM)
    r = np.arange(rows)
    row_start = np.clip(r - wr // 2, 0, rows - wr).astype(np.int32)
    row_off_idx = (row_start[:, None] + np.arange(wr)[None, :] - r[:, None] + NA_MAX_ROWS - 1).astype(np.int32)
    c = np.arange(GRID_W)
    col_start = np.clip(c - NA_COLS // 2, 0, GRID_W - NA_COLS)
    in_win = (c[None, :] >= col_start[:, None]) & (c[None, :] < col_start[:, None] + NA_COLS)
    col_idx = np.clip(c[None, :] - c[:, None], -(NA_COLS - 1), NA_COLS - 1) + NA_COLS - 1
    col_mask = jnp.asarray(in_win)
    rpb_cols = rpb[:, :, col_idx]
    scale = HEAD_DIM ** -0.5

    def row_block(args):
        q_row, start, off_idx = args
        k_band = lax.dynamic_slice_in_dim(kg, start, wr, axis=1)
        v_band = lax.dynamic_slice_in_dim(vg, start, wr, axis=1)
        s = jnp.einsum('bqhd,bwkhd->bhqwk', q_row, k_band).astype(jnp.float32) * scale
        bias = jnp.take(rpb_cols, off_idx, axis=1).transpose(0, 2, 1, 3)
        s = s + bias.astype(jnp.float32)[None]
        s = jnp.where(col_mask[None, None, :, None, :], s, -jnp.inf)
        p = jax.nn.softmax(s, axis=(-2, -1)).astype(v_band.dtype)
        return jnp.einsum('bhqwk,bwkhd->bqhd', p, v_band)

    o = lax.map(row_block, (qg, jnp.asarray(row_start), jnp.asarray(row_off_idx)))
    return o.transpose(1, 0, 2, 3, 4).reshape(B, S, WIDTH_B)


def depthwise_conv_centred(u, w, b):
    pad = CONV_WIDTH // 2
    S = u.shape[1]
    up = jnp.pad(u, ((0, 0), (pad, pad), (0, 0)))
    out = up[:, 0:S] * w[0] + b
    for j in range(1, CONV_WIDTH):
        out = out + up[:, j:j + S] * w[j]
    return out


def encoder_layer(x, pre_mix_norm, w_in, b_gate, q_norm, k_norm, rpb, w_proj_a, w_proj_b, w_out,
                  post_mix_norm, pre_ffn_norm, w_up, conv_w, conv_b, w_down, post_ffn_norm):
    h = rms_norm(x, pre_mix_norm)
    proj = h @ w_in
    qa, ka, va, qb, kb, vb, g_a, g_b = jnp.split(proj, SPLIT_POINTS, axis=-1)
    gate_a = jax.nn.sigmoid((g_a + b_gate[:D_MODEL]).astype(jnp.float32)).astype(x.dtype)
    gate_b = jax.nn.sigmoid((g_b + b_gate[D_MODEL:]).astype(jnp.float32)).astype(x.dtype)
    o_a = gqa_branch(qa, ka, va, q_norm, k_norm) @ w_proj_a
    o_b = neighbourhood_branch(qb, kb, vb, rpb) @ w_proj_b
    mix = (gate_a * o_a + gate_b * o_b) @ w_out
    x = x + rms_norm(mix, post_mix_norm)
    h = rms_norm(x, pre_ffn_norm)
    u = depthwise_conv_centred(h @ w_up, conv_w, conv_b)
    gate, val = u[..., :D_FF], u[..., D_FF:]
    f = (jax.nn.gelu(gate, approximate=True) * val) @ w_down
    return x + rms_norm(f, post_ffn_norm)


def setup_inputs(seed: int = 0) -> dict:
    key = jax.random.key(seed)
    ks = jax.random.split(key, 20)
    f32 = jnp.float32

    def nrm(k, shape, scale):
        return jax.random.normal(k, shape, f32) * scale

    def gain(k, shape):
        return 1.0 + 0.01 * jax.random.normal(k, shape, f32)

    return {
        "x_prompt": nrm(ks[0], (BATCH, SEQ, D_MODEL), 1.0),
        "x_sample": nrm(ks[1], (DEC_BATCH, DEC_SEQ, D_MODEL), 1.0),
        "pre_mix_norm": gain(ks[2], (DEPTH, D_MODEL)),
        "w_in": nrm(ks[3], (DEPTH, D_MODEL, D_IN), D_MODEL ** -0.5),
        "b_gate": nrm(ks[4], (DEPTH, 2 * D_MODEL), 0.01),
        "q_norm": gain(ks[5], (DEPTH, HEAD_DIM)),
        "k_norm": gain(ks[6], (DEPTH, HEAD_DIM)),
        "rpb": nrm(ks[7], (DEPTH, N_HEADS_B, 2 * NA_MAX_ROWS - 1, 2 * NA_COLS - 1), 0.02),
        "w_proj_a": nrm(ks[8], (DEPTH, WIDTH_A, D_MODEL), WIDTH_A ** -0.5),
        "w_proj_b": nrm(ks[9], (DEPTH, WIDTH_B, D_MODEL), WIDTH_B ** -0.5),
        "w_out": nrm(ks[10], (DEPTH, D_MODEL, D_MODEL), D_MODEL ** -0.5),
        "post_mix_norm": gain(ks[11], (DEPTH, D_MODEL)),
        "pre_ffn_norm": gain(ks[12], (DEPTH, D_MODEL)),
        "w_up": nrm(ks[13], (DEPTH, D_MODEL, 2 * D_FF), D_MODEL ** -0.5),
        "conv_w": nrm(ks[14], (DEPTH, CONV_WIDTH, 2 * D_FF), CONV_WIDTH ** -0.5),
        "conv_b": nrm(ks[15], (DEPTH, 2 * D_FF), 0.01),
        "w_down": nrm(ks[16], (DEPTH, D_FF, D_MODEL), D_FF ** -0.5),
        "post_ffn_norm": gain(ks[17], (DEPTH, D_MODEL)),
    }


def reference(x_prompt, x_sample, pre_mix_norm, w_in, b_gate, q_norm, k_norm, rpb, w_proj_a, w_proj_b,
              w_out, post_mix_norm, pre_ffn_norm, w_up, conv_w, conv_b, w_down, post_ffn_norm):
    def run_trunk(x):
        for l in range(DEPTH):
            x = encoder_layer(x, pre_mix_norm[l], w_in[l], b_gate[l], q_norm[l], k_norm[l], rpb[l],
                              w_proj_a[l], w_proj_b[l], w_out[l], post_mix_norm[l], pre_ffn_norm[l],
                              w_up[l], conv_w[l], conv_b[l], w_down[l], post_ffn_norm[l])
        return x

    y_prompt = run_trunk(x_prompt)
    y_sample = run_trunk(x_sample)
    return (y_prompt, y_sample)
```

```python
import contextlib
import numpy as np
import ml_dtypes
import concourse.bass as bass
import concourse.mybir as mybir
from concourse.bass_utils import run_bass_kernel_spmd

F32 = mybir.dt.float32
BF16 = mybir.dt.bfloat16
AF = mybir.ActivationFunctionType
ALU = mybir.AluOpType

D = 1024
DIN = 4352
DFF = 2816
EPS = 1e-6
NEG = -30000.0
NCORES = 8


class Buf:
    __slots__ = ("name", "writers", "readers")

    def __init__(self, name):
        self.name = name
        self.writers = {}
        self.readers = {}


class Op:
    __slots__ = ("q", "key", "fn", "deps", "signal", "ordinal", "is_dma")

    def __init__(self, q, key, fn, is_dma):
        self.q = q
        self.key = key
        self.fn = fn
        self.deps = []
        self.signal = is_dma
        self.ordinal = None
        self.is_dma = is_dma


ENGINES = ("pe", "act", "dve", "pool", "sp")


class Sched:
    def __init__(self, nc, n_chan=8):
        self.nc = nc
        self.ops = {e: [] for e in ENGINES}
        self.n_chan = n_chan
        self.chan_last = {}
        self.chan_rr = {}
        self.last_by_key = {}
        self.pending_barrier = {e: None for e in ENGINES}

    def barrier(self):
        snap = dict(self.last_by_key)
        for e in ENGINES:
            self.pending_barrier[e] = snap

    def _add(self, op, reads, writes):
        deps = {}
        for b in reads:
            for w in b.writers.values():
                deps[id(w)] = (w, "raw")
        for b in writes:
            for w in b.writers.values():
                deps[id(w)] = (w, "waw")
            for r in b.readers.values():
                if id(r) not in deps:
                    deps[id(r)] = (r, "war")
        pb = self.pending_barrier[op.q]
        if pb is not None:
            for d in pb.values():
                if id(d) not in deps:
                    deps[id(d)] = (d, "raw")
            self.pending_barrier[op.q] = None
        for d, typ in deps.values():
            if d is op:
                continue
            if (not op.is_dma) and (not d.is_dma) and d.key == op.key:
                if op.key == "pe":
                    continue
                if typ == "war":
                    continue
            op.deps.append(d)
            d.signal = True
        for b in reads:
            b.readers[op.key] = op
        for b in writes:
            b.writers = {op.key: op}
            b.readers = {}
        self.ops[op.q].append(op)
        self.last_by_key[op.key] = op
        return op

    def op(self, eng, fn, reads=(), writes=()):
        return self._add(Op(eng, eng, fn, False), reads, writes)

    def dma(self, q, fn, reads=(), writes=(), group="g"):
        rr = self.chan_rr.get((q, group), 0)
        self.chan_rr[(q, group)] = rr + 1
        key = "dma_%s_%s_%d" % (q, group, rr % self.n_chan)
        o = Op(q, key, fn, True)
        prev = self.chan_last.get(key)
        if prev is not None:
            o.deps.append(prev)
        self.chan_last[key] = o
        return self._add(o, reads, writes)

    def emit(self):
        nc = self.nc
        counts = {}
        for e in ENGINES:
            for o in self.ops[e]:
                if o.signal:
                    inc = 16 if o.is_dma else 1
                    counts[o.key] = counts.get(o.key, 0) + inc
                    o.ordinal = counts[o.key]
        keys = sorted(counts.keys())
        sems = {}
        with contextlib.ExitStack() as st:
            for k in keys:
                sems[k] = st.enter_context(nc.semaphore("s_" + k))
            block = st.enter_context(nc.Block())
            engmap = {"pe": block.tensor, "act": block.scalar, "dve": block.vector,
                      "pool": block.gpsimd, "sp": block.sync}
            final_waits = {k: counts[k] for k in keys if k.startswith("dma_")}

            def make(ename):
                ops = self.ops[ename]

                def body(eng):
                    waited = {}
                    for o in ops:
                        need = {}
                        for d in o.deps:
                            v = d.ordinal
                            if waited.get(d.key, 0) >= v:
                                continue
                            if need.get(d.key, 0) < v:
                                need[d.key] = v
                        for k, v in need.items():
                            eng.wait_ge(sems[k], v)
                            waited[k] = v
                        ins = o.fn(eng)
                        if o.signal:
                            ins.then_inc(sems[o.key], 16 if o.is_dma else 1)
                    if ename == "sp":
                        for k, v in final_waits.items():
                            if waited.get(k, 0) < v:
                                eng.wait_ge(sems[k], v)
                return body

            for e in ENGINES:
                if self.ops[e] or e == "sp":
                    engmap[e](make(e))
        return counts


def prompt_na_geometry():
    rows = []
    cols = []
    for r in range(32):
        start = min(max(r - 4, 0), 24)
        c0, c1 = start // 2, (start + 7) // 2
        lst = []
        for ci in range(c0, c1 + 1):
            m = 2 * ci - r + 7
            assert 0 <= m <= 13
            col = np.zeros(128, np.float32)
            for a in range(2):
                ok = start <= 2 * ci + a < start + 8
                if not ok:
                    col[a * 64:(a + 1) * 64] = NEG
                else:
                    assert m + a <= 14
            lst.append((ci, m, len(cols)))
            cols.append(col)
        rows.append(lst)
    return rows, np.stack(cols, axis=1)


def sample_na_geometry():
    def band(c, l):
        R = 32 * c - 6 + l
        if 0 <= R < 128:
            sg = min(max(R - 4, 0), 120)
        else:
            sg = R - 4
        sl = sg - 32 * c + 6
        return sl
    rows = []
    ncol = 0
    per_core_cols = [[] for _ in range(4)]
    for l in range(5, 39):
        starts = [band(c, l) for c in range(4)]
        lo = min(starts)
        hi = max(starts) + 8
        c0, c1 = lo // 2, (hi - 1) // 2
        lst = []
        for ci in range(c0, c1 + 1):
            m = 2 * ci - l + 7
            assert 0 <= m <= 13, (l, ci, m)
            assert 0 <= ci < 22
            for c in range(4):
                col = np.zeros(128, np.float32)
                for a in range(2):
                    ok = starts[c] <= 2 * ci + a < starts[c] + 8
                    if not ok:
                        col[a * 64:(a + 1) * 64] = NEG
                    else:
                        assert m + a <= 14
                per_core_cols[c].append(col)
            lst.append((ci, m, ncol))
            ncol += 1
        rows.append(lst)
    masks = [np.stack(cc, axis=1) for cc in per_core_cols]
    return rows, masks


def rope_tables(npos):
    t = np.arange(npos)
    row = (t // 64).astype(np.float32)
    col = (t % 64).astype(np.float32)
    half = 32
    freqs = (10000.0 ** (-np.arange(0, half, 2, dtype=np.float32) / half)).astype(np.float32)
    ang_r = row[:, None] * freqs[None, :]
    ang_c = col[:, None] * freqs[None, :]
    ang = np.concatenate([ang_r, ang_r, ang_c, ang_c], axis=-1).astype(np.float32)
    cos = np.cos(ang).astype(np.float32).T
    sin = np.sin(ang).astype(np.float32).T
    return np.concatenate([cos, cos], 0), np.concatenate([sin, sin], 0)


PROMPT_ROWS, PROMPT_MASK = prompt_na_geometry()
SAMPLE_ROWS, SAMPLE_MASKS = sample_na_geometry()
NMP = PROMPT_MASK.shape[1]
NMS = SAMPLE_MASKS[0].shape[1]


class Builder:
    def __init__(self, units=("p0", "p1", "p2", "p3", "s"), debug=False, stop=None):
        self.units = units
        self.debug = debug
        self.stop = stop
        self.nc = bass.Bass("TRN2", target_bir_lowering=False)
        self.S = Sched(self.nc)
        self.sb_off = 16512
        self.bufs = {}
        self.build()

    def sb(self, name, shape, dtype, off=None):
        esz = 4 if dtype == F32 else 2
        n = esz
        for s in shape[1:]:
            n *= s
        n = (n + 31) // 32 * 32
        if off is None:
            off = self.sb_off
            self.sb_off += n
            assert self.sb_off <= 229376, ("SBUF overflow", name, self.sb_off)
        t = self.nc.alloc_sbuf_tensor_at(name, list(shape), dtype, offset=off)
        return t

    def B(self, name):
        b = self.bufs.get(name)
        if b is None:
            b = Buf(name)
            self.bufs[name] = b
        return b

    def dram(self, name, shape, dtype, kind="Internal"):
        if self.debug and kind == "Internal":
            kind = "ExternalOutput"
        return self.nc.dram_tensor(name, list(shape), dtype, kind=kind).ap()

    def mm(self, out, lhsT, rhs, start, stop, rd, wr, skip=False):
        if skip:
            self.S.op("pe", lambda e: e.matmul(out, lhsT, rhs, start=start, stop=stop, skip_group_check=True), rd, wr)
        else:
            self.S.op("pe", lambda e: e.matmul(out, lhsT, rhs, start=start, stop=stop), rd, wr)

    def tr(self, out, in_, rd, wr):
        idn = self.ident[:, :]
        self.S.op("pe", lambda e: e.transpose(out, in_, idn), list(rd) + [self.B("ident")], wr)

    def act(self, out, in_, func, rd, wr, bias=None, scale=None, accum=None):
        kw = {}
        if bias is not None:
            kw["bias"] = bias
        if scale is not None:
            kw["scale"] = scale
        if accum is not None:
            kw["accum_out"] = accum
        self.S.op("act", lambda e: e.activation(out=out, in_=in_, func=func, **kw), rd, wr)

    def tt(self, eng, out, in0, in1, op, rd, wr):
        self.S.op(eng, lambda e: e.tensor_tensor(out, in0, in1, op), rd, wr)

    def ts(self, eng, out, in0, s1, s2, op0, op1, rd, wr):
        if s2 is None:
            self.S.op(eng, lambda e: e.tensor_scalar(out, in0, s1, None, op0), rd, wr)
        else:
            self.S.op(eng, lambda e: e.tensor_scalar(out, in0, s1, s2, op0, op1), rd, wr)

    def stt(self, eng, out, in0, scalar, in1, op0, op1, rd, wr):
        self.S.op(eng, lambda e: e.scalar_tensor_tensor(out, in0, scalar, in1, op0, op1), rd, wr)

    def act_rsqrt(self, out, in_, scale, rd, wr):
        self.act(out, in_, AF.Ln, list(rd) + [self.B("epsc")], wr, bias=self.epsc[:, 0:1], scale=scale)
        self.act(out, out, AF.Exp, wr, wr, scale=-0.5)

    def act_recip(self, out, in_, rd, wr):
        self.act(out, in_, AF.Ln, rd, wr)
        self.act(out, out, AF.Exp, wr, wr, scale=-1.0)

    def recip(self, out, in_, rd, wr):
        self.S.op("dve", lambda e: e.reciprocal(out, in_), rd, wr)

    def cp(self, eng, out, in_, rd, wr):
        if eng == "act":
            self.act(out, in_, AF.Copy, rd, wr)
        else:
            self.S.op(eng, lambda e: e.tensor_copy(out, in_), rd, wr)

    def memset(self, eng, ap, val, wr):
        self.S.op(eng, lambda e: e.memset(ap, val), (), wr)

    def load(self, out, in_, rd, wr, q="sp", group="g", slow=False):
        if slow:
            self.S.dma(q, lambda e: e.dma_start(out=out, in_=in_, allow_slow_non_contiguous=True), rd, wr, group=group)
        else:
            self.S.dma(q, lambda e: e.dma_start(out=out, in_=in_), rd, wr, group=group)

    def accum_store(self, out, in_, rd, wr):
        self.S.dma("pool", lambda e: e.dma_start(out=out, in_=in_, accum_op=ALU.add), rd, wr, group="acc")

    def bank(self, b):
        return self.PS[b // 2][:, (b % 2) * 512:(b % 2) * 512 + 512]

    def Bk(self, b):
        return self.B("bank%d" % b)

    def build(self):
        nc = self.nc
        din = lambda n, s, d=F32: nc.dram_tensor(n, list(s), d, kind="ExternalInput").ap()
        self.xp = din("xp", [4, 2048, D])
        self.xsf = din("xsf", [8192, D])
        self.xsl = din("xsl", [2816, D])
        self.yp = nc.dram_tensor("yp", [4, 2048, D], F32, kind="ExternalOutput").ap()
        self.ys = nc.dram_tensor("ys", [2048, D], F32, kind="ExternalOutput").ap()
        w_in = din("w_in", [D, DIN])
        w_pa = din("w_pa", [512, D])
        w_pb = din("w_pb", [512, D])
        w_out = din("w_out", [D, D])
        w_up = din("w_up", [D, 2 * DFF])
        w_dn = din("w_dn", [DFF, D])
        c_gpre = din("c_gpre", [128, 8])
        c_gpf = din("c_gpf", [128, 8])
        c_gpm = din("c_gpm", [128, D])
        c_gpo = din("c_gpo", [128, D])
        c_bg = din("c_bg", [128, 16])
        c_qkw = din("c_qkw", [128, 2])
        c_conv = din("c_conv", [128, 44 * 4])
        c_wt = din("c_wt", [128, 8 * 15 * 64])
        c_cm = din("c_cm", [128, 64])
        c_mp = din("c_mp", [128, NMP])
        c_ms = din("c_ms", [128, NMS])
        c_flags = din("c_flags", [128, 2])
        c_ident = din("c_ident", [128, 128], BF16)
        c_bones = din("c_bones", [128, 128])
        c_rotm = din("c_rotm", [128, 128])
        self.ropeK = (din("c_cosk", [128, 8192]), din("c_sink", [128, 8192]))
        self.ropeQs = (din("c_cosq", [128, 2176]), din("c_sinq", [128, 2176]))

        self.wqa_s = self.dram("wqa_s", [128, 8, 512], BF16)
        self.wqb_s = self.dram("wqb_s", [128, 8, 512], BF16)
        self.wk_s = self.dram("wk_s", [128, 8, 1280], BF16)
        self.wg_s = self.dram("wg_s", [4, 128, 8, 512], BF16)
        self.wpa_s = self.dram("wpa_s", [128, 4, 1024], BF16)
        self.wpb_s = self.dram("wpb_s", [128, 4, 1024], BF16)
        self.wo_s = self.dram("wo_s", [2, 128, 8, 512], BF16)
        self.wup_s = self.dram("wup_s", [11, 128, 8, 512], BF16)
        self.wdn_s = self.dram("wdn_s", [128, 22, 1024], BF16)

        S = self.S
        Bf = self.B
        self.ident = self.sb("ident", [128, 128], BF16)
        self.bones = self.sb("bones", [128, 128], F32)
        self.rotm = self.sb("rotm", [128, 128], F32)
        self.gpre = self.sb("gpre", [128, 8], F32)
        self.gpf = self.sb("gpf", [128, 8], F32)
        self.gpm = self.sb("gpm", [128, D], F32)
        self.gpo = self.sb("gpo", [128, D], F32)
        self.bg = self.sb("bg", [128, 16], F32)
        self.qkw = self.sb("qkw", [128, 2], F32)
        self.conv = self.sb("conv", [128, 44, 4], F32)
        self.wt = self.sb("wt", [128, 8, 15, 64], BF16)
        self.cm = self.sb("cm", [128, 64], F32)
        self.mp = self.sb("mp", [128, NMP], F32)
        self.ms = self.sb("ms", [128, NMS], F32)
        self.flags = self.sb("flags", [128, 2], F32)
        self.epsc = self.sb("epsc", [128, 1], F32)
        self.zero_bf = self.sb("zero_bf", [128, 8], BF16)
        cl = [(self.ident, c_ident, "ident"), (self.bones, c_bones, "bones"), (self.rotm, c_rotm, "rotm"),
              (self.gpre, c_gpre, "gpre"), (self.gpf, c_gpf, "gpf"), (self.gpm, c_gpm, "gpm"),
              (self.gpo, c_gpo, "gpo"), (self.bg, c_bg, "bg"), (self.qkw, c_qkw, "qkw"),
              (self.cm, c_cm, "cm"), (self.mp, c_mp, "mp"), (self.ms, c_ms, "ms"), (self.flags, c_flags, "flags")]
        for t, src, nm in cl:
            self.load(t[:, :], src, (), [Bf(nm)])
        self.load(self.conv[:, :, :], c_conv.rearrange("p (c k) -> p c k", k=4), (), [Bf("conv")])
        self.load(self.wt[:, :, :, :], c_wt.rearrange("p (h m q) -> p h m q", h=8, m=15), (), [Bf("wt")], q="pool", group="cv")
        self.memset("pool", self.epsc[:, :], EPS, [Bf("epsc")])
        self.memset("pool", self.zero_bf[:, :], 0.0, [Bf("zero_bf")])
        wt3 = self.wt[:, :, :, :].rearrange("p h m q -> p (h m) q")
        for i in range(4):
            sl = slice(i * 30, (i + 1) * 30)
            self.tt("dve", wt3[:, sl, :], wt3[:, sl, :], self.cm[:, :].unsqueeze(1).broadcast_to([128, 30, 64]),
                    ALU.add, [Bf("wt"), Bf("cm")], [Bf("wt")])

        self.pending_conv = []
        self.pending_late = []
        self.resident_loaded = False

        def conv_now(dst, src, nm):
            S.dma("pool", lambda e: e.dma_start(out=dst, in_=src), (), [Bf(nm)], group="cv")

        def conv_dma(dst, src, nm):
            if nm == "wk_s":
                conv_now(dst, src, nm)
            elif nm in ("wup_s", "wdn_s"):
                self.pending_late.append(lambda: conv_now(dst, src, nm))
            else:
                self.pending_conv.append(lambda: conv_now(dst, src, nm))

        conv_dma(self.wk_s[:, :, 0:256], w_in[:, 512:768].rearrange("(k p) n -> p k n", p=128), "wk_s")
        conv_dma(self.wk_s[:, :, 256:768], w_in[:, 1280:1792].rearrange("(k p) n -> p k n", p=128), "wk_s")
        conv_dma(self.wk_s[:, :, 768:1280], w_in[:, 1792:2304].rearrange("(k p) n -> p k n", p=128), "wk_s")
        for j in range(4):
            for g in range(2):
                h = g * 4 + j
                conv_dma(self.wqa_s[:, :, j * 128 + g * 64:j * 128 + g * 64 + 64],
                         w_in[:, h * 64:(h + 1) * 64].rearrange("(k p) n -> p k n", p=128), "wqa_s")
        conv_dma(self.wqb_s, w_in[:, 768:1280].rearrange("(k p) n -> p k n", p=128), "wqb_s")
        for pn in range(4):
            conv_dma(self.wg_s[pn, :, :, 0:256], w_in[:, 2304 + pn * 256:2304 + (pn + 1) * 256].rearrange("(k p) n -> p k n", p=128), "wg_s")
            conv_dma(self.wg_s[pn, :, :, 256:512], w_in[:, 3328 + pn * 256:3328 + (pn + 1) * 256].rearrange("(k p) n -> p k n", p=128), "wg_s")
        for g in range(2):
            conv_dma(self.wpa_s[g * 64:(g + 1) * 64, :, :],
                     w_pa[g * 256:(g + 1) * 256, :].rearrange("(j p) n -> p j n", p=64), "wpa_s")
        conv_dma(self.wpb_s, w_pb.rearrange("(k p) n -> p k n", p=128), "wpb_s")
        for hf in range(2):
            conv_dma(self.wo_s[hf], w_out[:, hf * 512:(hf + 1) * 512].rearrange("(k p) n -> p k n", p=128), "wo_s")
        for pn in range(11):
            conv_dma(self.wup_s[pn, :, :, 0:256], w_up[:, pn * 256:(pn + 1) * 256].rearrange("(k p) n -> p k n", p=128), "wup_s")
            conv_dma(self.wup_s[pn, :, :, 256:512], w_up[:, DFF + pn * 256:DFF + (pn + 1) * 256].rearrange("(k p) n -> p k n", p=128), "wup_s")
        for q4 in range(2):
            conv_dma(self.wdn_s[:, q4 * 11:(q4 + 1) * 11, :],
                     w_dn[q4 * 1408:(q4 + 1) * 1408, :].rearrange("(k p) n -> p k n", p=128), "wdn_s")

        self.PS = [nc.alloc_psum_tensor("ps%d" % i, [128, 1024], F32) for i in range(4)]

        self.WR = [self.sb("wr%d" % i, [128, 8, 512], BF16) for i in range(4)]
        self.wpa_t = self.sb("wpa_t", [128, 4, 1024], BF16)
        self.wpb_t = self.sb("wpb_t", [128, 4, 1024], BF16)

        self.wr_i = 0
        self.xt = self.sb("xt", [128, 4, D], F32)
        self.xn = [self.sb("xn%d" % i, [128, D], BF16) for i in range(2)]
        self.hT = self.sb("hT", [128, 8, 512], BF16)
        self.ss = self.sb("ss", [128, 8], F32)
        self.rstd = self.sb("rstd", [128, 8], F32)
        self.tmp = [self.sb("tmp%d" % i, [128, 512], F32) for i in range(6)]
        self.tmp_i = 0
        self.ropec = self.sb("ropec", [128, 512], F32)
        self.ropes = self.sb("ropes", [128, 512], F32)
        base = self.sb_off
        self.qaz = [self.sb("qaz%d" % g, [128, 4, 512], BF16) for g in range(2)]
        self.qz = [self.sb("qz%d" % a, [128, 4, 512], BF16) for a in range(2)]
        self.oaT = self.sb("oaT", [128, 4, 512], BF16)
        self.obT = self.sb("obT", [128, 4, 512], BF16)
        self.mT = self.sb("mT", [128, 8, 512], BF16)
        self.PT = [self.sb("pt%d" % i, [128, 2, 512], BF16) for i in range(4)]
        self.pt_i = 0
        self.kring = [self.sb("kring%d" % i, [128, 1024], BF16) for i in range(2)]
        self.vring = [self.sb("vring%d" % i, [128, 8, 128], BF16) for i in range(2)]
        self.kv_i = 0
        kbb_off = self.sb_off
        self.kbb = self.sb("kbb", [128, 4, 9 * 128], BF16)
        self.vbb = self.sb("vbb", [128, 9, 768], BF16)
        mix_end = self.sb_off
        self.sb_off = base
        self.wk = self.sb("wk", [128, 8, 1280], BF16)
        self.kat = self.sb("kat", [128, 512], BF16)
        self.vat = self.sb("vat", [128, 4, 192], BF16)
        self.kbt = self.sb("kbt", [128, 4, 512], BF16)
        self.vbt = self.sb("vbt", [128, 4, 768], BF16)
        self.xt_b = self.sb("xt_b", [128, 4, D], F32)
        self.hT_b = self.sb("hT_b", [128, 8, 512], BF16)
        pre_end = self.sb_off
        self.sb_off = max(mix_end, pre_end)
        self.hw = self.sb("hw", [128, 8, 514], BF16)
        self.fo = [self.sb("fo%d" % i, [128, D], F32) for i in range(2)]
        self.gT = self.sb("gT", [128, 22, 512], BF16, off=kbb_off)
        assert 4 * 9 * 128 * 2 + 9 * 768 * 2 >= 22 * 512 * 2
        print("SBUF used:", self.sb_off, "of 229376")

        self.unit = {}
        for u in self.units:
            samp = (u == "s")
            SA = 8192 if samp else 2048
            RL = 44 if samp else 32
            NQ = 2176 if samp else 2048
            d = dict(
                samp=samp, SA=SA, RL=RL, NQ=NQ,
                kaT=self.dram("kaT_" + u, [128, SA], BF16),
                va=self.dram("va_" + u, [128, SA // 128, 192], BF16),
                kbT=self.dram("kbT_" + u, [128, 4, RL * 64], BF16),
                vb=self.dram("vb_" + u, [128, RL // 2, 768], BF16),
                h2T=self.dram("h2T_" + u, [128, 8, NQ + 2], BF16),
            )
            if samp:
                d.update(xA=self.xsf, xL=self.xsl, y=self.ys, qrow0=5, own0=6, rows=SAMPLE_ROWS, mask=self.ms, maskbuf="ms",
                         ropeQ=self.ropeQs, mix_tiles=[(5, 8), (13, 8), (21, 8), (29, 8), (37, 2)])
            else:
                i = int(u[1])
                d.update(xA=self.xp[i], xL=self.xp[i], y=self.yp[i], qrow0=0, own0=0, rows=PROMPT_ROWS, mask=self.mp, maskbuf="mp",
                         ropeQ=self.ropeK, mix_tiles=[(0, 8), (8, 8), (16, 8), (24, 8)])
            self.unit[u] = d

        for ui, u in enumerate(self.units):
            U = self.unit[u]
            if self.stop == "conv":
                break
            if ui > 0:
                S.barrier()
            self.prepass(u, U)
            S.barrier()
            if self.stop == "pre":
                break
            self.memset("pool", self.qz[0][64:128, :, :], 0.0, [Bf("qz")])
            self.memset("pool", self.qz[1][0:64, :, :], 0.0, [Bf("qz")])
            self.memset("pool", self.qaz[0][64:128, :, :], 0.0, [Bf("qaz")])
            self.memset("pool", self.qaz[1][0:64, :, :], 0.0, [Bf("qaz")])
            for colx in (0, U["NQ"] + 1):
                self.load(U["h2T"][:, :, colx:colx + 1], self.zero_bf[:, :].unsqueeze(2), [Bf("zero_bf")], [Bf("h2T_" + u)], slow=True)
            nt = len(U["mix_tiles"])
            if self.stop is not None and self.stop.startswith("mix"):
                self.mix_tile(u, U, 0)
                break
            if self.stop == "ffn0":
                self.mix_tile(u, U, 0, prefetch_next=True)
                self.mix_tile(u, U, 1, preloaded=True)
                self.ffn_tile(u, U, 0)
                break
            front_done = False
            for ti in range(nt):
                if ti >= 2:
                    self.ffn_preload(u, U, ti - 2)
                self.mix_tile(u, U, ti, preloaded=(ti > 0), front_done=front_done, defer_h2=True)

                def btw(ti=ti):
                    self.mix_h2(u, U, ti, prefetch_next=(ti + 1 < nt))
                    if ti + 1 < nt:
                        self.mix_tile(u, U, ti + 1, preloaded=True, only_front=True)
                if ti >= 2:
                    self.ffn_tile(u, U, ti - 2, preloaded=True, between=btw)
                else:
                    btw()
                front_done = (ti + 1 < nt)
            for fi in range(max(nt - 2, 0), 4):
                self.ffn_tile(u, U, fi)
        self.counts = S.emit()

    def trickle_late(self, n):
        for _ in range(n):
            if self.pending_late:
                self.pending_late.pop(0)()

    def next_tmp(self):
        i = self.tmp_i % len(self.tmp)
        self.tmp_i += 1
        return self.tmp[i], self.B("tmp%d" % i)

    def next_wr(self):
        i = self.wr_i % len(self.WR)
        self.wr_i += 1
        return self.WR[i], self.B("wr%d" % i)

    def next_pt(self):
        i = self.pt_i % len(self.PT)
        self.pt_i += 1
        return self.PT[i], self.B("pt%d" % i)

    def norm_transpose(self, ns, gain, gain_buf, out, out_buf, out_col0=0, xt=None, xtn="xt"):
        Bf = self.B
        if xt is None:
            xt = self.xt
        self.memset("pool", self.ss[:, 0:ns], 0.0, [Bf("ss")])
        for s in range(ns):
            xn = self.xn[s % 2]
            self.act(xn[:, :], xt[:, s, :], AF.Square, [Bf(xtn), Bf("ss")], [Bf("xn%d" % (s % 2)), Bf("ss")],
                     accum=self.ss[:, s:s + 1])
        self.act_rsqrt(self.rstd[:, 0:ns], self.ss[:, 0:ns], 1.0 / D, [Bf("ss")], [Bf("rstd")])
        for s in range(ns):
            xn = self.xn[s % 2]
            xb = Bf("xn%d" % (s % 2))
            if s % 2 == 0:
                self.act(xn[:, :], xt[:, s, :], AF.Identity, [Bf(xtn), Bf("rstd")], [xb], scale=self.rstd[:, s:s + 1])
            else:
                self.ts("dve", xn[:, :], xt[:, s, :], self.rstd[:, s:s + 1], None, ALU.mult, None,
                        [Bf(xtn), Bf("rstd")], [xb])
            bk = s % 2
            pb = self.bank(bk).bitcast(BF16).rearrange("p (k t) -> p k t", k=8)
            for kc in range(8):
                self.tr(pb[:, kc, :], xn[:, kc * 128:(kc + 1) * 128], [xb], [self.Bk(bk)])
            c0 = out_col0 + s * 128
            self.tt("dve", out[:, :, c0:c0 + 128], pb[:, :, :], gain[:, :].unsqueeze(2).broadcast_to([128, 8, 128]),
                    ALU.mult, [self.Bk(bk), gain_buf], [out_buf])

    def load_x(self, src, ns, xt=None, xtn="xt"):
        if xt is None:
            xt = self.xt
        self.load(xt[:, 0:ns, :], src.rearrange("(s p) d -> p s d", p=128), (), [self.B(xtn)], group="x")

    def qk_norm_rope(self, ps_bank, NT, wcol, ropec, ropes, outs, out_buf, rope_bufs):
        Bf = self.B
        ps = self.bank(ps_bank)[:, 0:NT]
        sq, sqb = self.next_tmp()
        self.act(sq[:, 0:NT], ps, AF.Square, [self.Bk(ps_bank)], [sqb])
        self.mm(self.bank(4)[:, 0:NT], self.bones[:, :], sq[:, 0:NT], True, True, [sqb, Bf("bones")], [self.Bk(4)])
        rt, rtb = self.next_tmp()
        self.act_rsqrt(rt[:, 0:NT], self.bank(4)[:, 0:NT], 1.0 / 64, [self.Bk(4)], [rtb])
        qn, qnb = self.next_tmp()
        self.stt("dve", qn[:, 0:NT], ps, self.qkw[:, wcol:wcol + 1], rt[:, 0:NT], ALU.mult, ALU.mult,
                 [self.Bk(ps_bank), rtb, Bf("qkw")], [qnb])
        self.mm(self.bank(5)[:, 0:NT], self.rotm[:, :], qn[:, 0:NT], True, True, [qnb, Bf("rotm")], [self.Bk(5)])
        t1, t1b = self.next_tmp()
        self.tt("pool", t1[:, 0:NT], qn[:, 0:NT], ropec, ALU.mult, [qnb] + rope_bufs, [t1b])
        t2, t2b = self.next_tmp()
        self.tt("dve", t2[:, 0:NT], self.bank(5)[:, 0:NT], ropes, ALU.mult, [self.Bk(5)] + rope_bufs, [t2b])
        for (psl, oap) in outs:
            self.tt("pool", oap, t1[psl, 0:NT], t2[psl, 0:NT], ALU.add, [t1b, t2b], [out_buf])

    def prepass(self, u, U):
        Bf = self.B
        samp = U["samp"]
        self.load(self.wk[:, :, :], self.wk_s, [Bf("wk_s")], [Bf("wk")])
        self.memset("pool", self.vat[:, :, 64:128], 1.0, [Bf("vat")])
        vbt5 = self.vbt[:, :, :].rearrange("p s (j c) -> p s j c", j=4)
        self.memset("pool", vbt5[:, :, :, 64:128], 1.0, [Bf("vbt")])
        tiles = []
        if samp:
            for i in range(16):
                tiles.append((U["xA"][i * 512:(i + 1) * 512, :], 4, True, i * 512, False, 0))
            for i in range(5):
                tiles.append((U["xL"][i * 512:(i + 1) * 512, :], 4, False, 0, True, i * 512))
            tiles.append((U["xL"][2560:2816, :], 2, False, 0, True, 2560))
        else:
            for i in range(4):
                tiles.append((U["xA"][i * 512:(i + 1) * 512, :], 4, True, i * 512, True, i * 512))
        xbufs = [(self.xt, "xt", self.hT, "hT"), (self.xt_b, "xt_b", self.hT_b, "hT_b")]
        self.load_x(tiles[0][0], tiles[0][1], xbufs[0][0], xbufs[0][1])
        def stageA(tix):
            (src, ns, doA, tA, doB, tB) = tiles[tix]
            xt_c, xtn_c, hT_c, hTn_c = xbufs[tix % 2]
            if tix + 1 < len(tiles):
                nb_ = xbufs[(tix + 1) % 2]
                self.load_x(tiles[tix + 1][0], tiles[tix + 1][1], nb_[0], nb_[1])
            self.norm_transpose(ns, self.gpre, Bf("gpre"), hT_c, Bf(hTn_c), xt=xt_c, xtn=xtn_c)

        def stageB(tix):
            (src, ns, doA, tA, doB, tB) = tiles[tix]
            NT = ns * 128
            xt_c, xtn_c, hT_c, hTn_c = xbufs[tix % 2]
            if doA:
                self.load(self.ropec[:, 0:NT], self.ropeK[0][:, tA:tA + NT], (), [Bf("ropec")], group="x")
                self.load(self.ropes[:, 0:NT], self.ropeK[1][:, tA:tA + NT], (), [Bf("ropes")], group="x")
                for kc in range(8):
                    self.mm(self.bank(2)[:, 0:NT], self.wk[:, kc, 0:128], hT_c[:, kc, 0:NT], kc == 0, kc == 7,
                            [Bf("wk"), Bf(hTn_c)], [self.Bk(2)])
                self.qk_norm_rope(2, NT, 1, self.ropec[:, 0:NT], self.ropes[:, 0:NT],
                                  [(slice(0, 128), self.kat[:, 0:NT])], Bf("kat"), [Bf("ropec"), Bf("ropes")])
                self.load(U["kaT"][:, tA:tA + NT], self.kat[:, 0:NT], [Bf("kat")], [Bf("kaT_" + u)], group="st")
                pv = self.bank(3).rearrange("p (s c) -> p s c", s=4)
                for s in range(ns):
                    for kc in range(8):
                        self.mm(pv[:, s, :], hT_c[:, kc, s * 128:(s + 1) * 128], self.wk[:, kc, 128:256],
                                kc == 0, kc == 7, [Bf("wk"), Bf(hTn_c)], [self.Bk(3)])
                vat4 = self.vat[:, :, :].rearrange("p s (a c) -> p s a c", a=3)
                pv4 = self.bank(3).rearrange("p (s a c) -> p s a c", s=4, a=2)
                self.cp("act", vat4[:, 0:ns, 0:3:2, :], pv4[:, 0:ns, :, :], [self.Bk(3)], [Bf("vat")])
                self.load(U["va"][:, tA // 128:tA // 128 + ns, :], self.vat[:, 0:ns, :], [Bf("vat")], [Bf("va_" + u)],
                          group="st")
            if doB:
                for j in range(4):
                    bk = 6 + (j % 2)
                    for kc in range(8):
                        self.mm(self.bank(bk)[:, 0:NT], self.wk[:, kc, 256 + j * 128:256 + (j + 1) * 128],
                                hT_c[:, kc, 0:NT], kc == 0, kc == 7, [Bf("wk"), Bf(hTn_c)], [self.Bk(bk)])
                    self.cp("act", self.kbt[:, j, 0:NT], self.bank(bk)[:, 0:NT], [self.Bk(bk)], [Bf("kbt")])
                self.load(U["kbT"][:, :, tB:tB + NT], self.kbt[:, :, 0:NT], [Bf("kbt")], [Bf("kbT_" + u)], group="st")
                for s in range(ns):
                    bk = 2 + (s % 2) if not doA else 6 + (s % 2)
                    for kc in range(8):
                        self.mm(self.bank(bk)[:, :], hT_c[:, kc, s * 128:(s + 1) * 128], self.wk[:, kc, 768:1280],
                                kc == 0, kc == 7, [Bf("wk"), Bf(hTn_c)], [self.Bk(bk)])
                    dst = self.vbt[:, s, :].rearrange("p (j a c) -> p j a c", j=4, a=3)[:, :, 0:3:2, :]
                    srcp = self.bank(bk).rearrange("p (j a c) -> p j a c", j=4, a=2)
                    self.cp("dve", dst, srcp, [self.Bk(bk)], [Bf("vbt")])
                self.load(U["vb"][:, tB // 128:tB // 128 + ns, :], self.vbt[:, 0:ns, :], [Bf("vbt")], [Bf("vb_" + u)],
                          group="st")


        stageA(0)
        for tix in range(len(tiles)):
            if tix + 1 < len(tiles):
                stageA(tix + 1)
            stageB(tix)
            for _ in range(12):
                if self.pending_conv:
                    self.pending_conv.pop(0)()
        while self.pending_conv:
            self.pending_conv.pop(0)()
        if not self.resident_loaded:
            self.resident_loaded = True
            self.load(self.wpa_t[:, :, :], self.wpa_s, [Bf("wpa_s")], [Bf("wpa_t")], group="w")
            self.load(self.wpb_t[:, :, :], self.wpb_s, [Bf("wpb_s")], [Bf("wpb_t")], group="w")

    def mix_x_loads(self, U, ti):
        row0, nr = U["mix_tiles"][ti]
        NT = nr * 64
        tl0 = row0 * 64
        self.load_x(U["xL"][tl0:tl0 + NT, :], NT // 128)

    def mix_h2(self, u, U, ti, prefetch_next):
        Bf = self.B
        row0, nr = U["mix_tiles"][ti]
        NT = nr * 64
        ns = NT // 128
        tq0 = (row0 - U["qrow0"]) * 64
        self.norm_transpose(ns, self.gpf, Bf("gpf"), self.hT, Bf("hT"))
        if prefetch_next:
            self.mix_x_loads(U, ti + 1)
        self.load(U["h2T"][:, :, 1 + tq0:1 + tq0 + NT], self.hT[:, :, 0:NT], [Bf("hT")], [Bf("h2T_" + u)], group="st")

    def mix_tile(self, u, U, ti, preloaded=False, prefetch_next=False, front_done=False, only_front=False,
                 defer_h2=False):
        Bf = self.B
        samp = U["samp"]
        row0, nr = U["mix_tiles"][ti]
        NT = nr * 64
        ns = NT // 128
        qrow0 = U["qrow0"]
        tl0 = row0 * 64
        tq0 = (row0 - qrow0) * 64
        if not front_done:
            if not preloaded:
                self.load_x(U["xL"][tl0:tl0 + NT, :], ns)
            self.load(self.ropec[:, 0:NT], U["ropeQ"][0][:, tq0:tq0 + NT], (), [Bf("ropec")], group="x")
            self.load(self.ropes[:, 0:NT], U["ropeQ"][1][:, tq0:tq0 + NT], (), [Bf("ropes")], group="x")
            self.norm_transpose(ns, self.gpre, Bf("gpre"), self.hT, Bf("hT"))
        if only_front:
            return

        wqa, wqab = self.next_wr()
        self.load(wqa[:, :, :], self.wqa_s, [Bf("wqa_s")], [wqab], group="w")
        wqb, wqbb = self.next_wr()
        self.load(wqb[:, :, :], self.wqb_s, [Bf("wqb_s")], [wqbb], group="w")
        def qb_group(j):
            bk = 6 + (j % 2)
            for kc in range(8):
                self.mm(self.bank(bk)[:, 0:NT], wqb[:, kc, j * 128:(j + 1) * 128], self.hT[:, kc, 0:NT], kc == 0, kc == 7,
                        [wqbb, Bf("hT")], [self.Bk(bk)])

        def qb_copy(j):
            bk = 6 + (j % 2)
            self.cp("act", self.qz[0][0:64, j, 0:NT], self.bank(bk)[0:64, 0:NT], [self.Bk(bk)], [Bf("qz")])
            self.cp("act", self.qz[1][64:128, j, 0:NT], self.bank(bk)[64:128, 0:NT], [self.Bk(bk)], [Bf("qz")])

        for j in range(4):
            for kc in range(8):
                self.mm(self.bank(j)[:, 0:NT], wqa[:, kc, j * 128:(j + 1) * 128], self.hT[:, kc, 0:NT], kc == 0, kc == 7,
                        [wqab, Bf("hT")], [self.Bk(j)])
        qb_group(0)
        qb_group(1)
        for j in range(4):
            self.qk_norm_rope(j, NT, 0, self.ropec[:, 0:NT], self.ropes[:, 0:NT],
                              [(slice(0, 64), self.qaz[0][0:64, j, 0:NT]), (slice(64, 128), self.qaz[1][64:128, j, 0:NT])],
                              Bf("qaz"), [Bf("ropec"), Bf("ropes")])
            if j == 0:
                qb_copy(0)
                qb_copy(1)
                qb_group(2)
                qb_group(3)
            if j == 1:
                qb_copy(2)
                qb_copy(3)
        self.trickle_late(4)
        if self.stop == "mix_q":
            return
        SA = U["SA"]
        nblk = SA // 1024
        for g in range(2):
            vcol = slice(0, 128) if g == 0 else slice(64, 192)
            pend = None
            for blk in range(nblk):
                ri = self.kv_i % 2
                self.kv_i += 1
                kr, vr = self.kring[ri], self.vring[ri]
                krb, vrb = Bf("kring%d" % ri), Bf("vring%d" % ri)
                self.load(kr[:, :], U["kaT"][:, blk * 1024:(blk + 1) * 1024], [Bf("kaT_" + u)], [krb], group="kv")
                self.load(vr[:, :, :], U["va"][:, blk * 8:(blk + 1) * 8, vcol], [Bf("va_" + u)], [vrb], group="kv")
                for c in range(8):
                    for j in range(4):
                        self.mm(self.bank(4 + j)[:, 0:NT], kr[:, c * 128:(c + 1) * 128], self.qaz[g][:, j, 0:NT], True, True,
                                [krb, Bf("qaz")], [self.Bk(4 + j)])
                    pts = []
                    for hp in range(2):
                        pt, ptb = self.next_pt()
                        src = self.PS[2 + hp][:, :].rearrange("p (b n) -> p b n", b=2)[:, :, 0:NT]
                        self.act(pt[:, :, 0:NT], src, AF.Exp, [self.Bk(4 + 2 * hp), self.Bk(5 + 2 * hp)], [ptb], scale=0.125)
                        pts.append((pt, ptb))
                    first = (blk == 0 and c == 0)
                    last = (blk == nblk - 1 and c == 7)
                    cur = (pts, vr, vrb, c, first, last)
                    if pend is not None:
                        self._gqa_pv(pend, NT)
                    pend = cur
            self._gqa_pv(pend, NT)
            os_ = slice(0, 64) if g == 0 else slice(64, 128)
            ds_ = slice(64, 128) if g == 0 else slice(0, 64)
            for j in range(4):
                rc, rcb = self.next_tmp()
                self.act_recip(rc[ds_, 0:NT], self.bank(j)[ds_, 0:NT], [self.Bk(j)], [rcb])
                self.tt("dve", self.oaT[os_, j, 0:NT], self.bank(j)[os_, 0:NT], rc[ds_, 0:NT], ALU.mult,
                        [self.Bk(j), rcb], [Bf("oaT")])
        self.trickle_late(4)
        if self.stop == "mix_gqa":
            return
        rows = U["rows"]
        qrows = [rows[(row0 - qrow0) + i] for i in range(nr)]
        cmin = min(ci for lst in qrows for (ci, m, mc) in lst)
        cmax = max(ci for lst in qrows for (ci, m, mc) in lst)
        nb = cmax - cmin + 1
        assert nb <= 9
        self.load(self.kbb[:, :, 0:nb * 128], U["kbT"][:, :, cmin * 128:(cmax + 1) * 128], [Bf("kbT_" + u)],
                  [Bf("kbb"), Bf("gT")], group="kv")
        self.load(self.vbb[:, 0:nb, :], U["vb"][:, cmin:cmax + 1, :], [Bf("vb_" + u)], [Bf("vbb"), Bf("gT")], group="kv")
        if self.stop == "mix_nal":
            return
        steps = []
        for i in range(nr):
            lst = qrows[i]
            for n_, (ci, m, mc) in enumerate(lst):
                steps.append((i, ci - cmin, m, mc, n_ == 0, n_ == len(lst) - 1))
        DEPTH = 3
        queue = []

        def na_finish(item):
            (i, lc, first, last, pt, ptb) = item
            ob = i % 2
            po = self.bank(ob).rearrange("p (h q) -> p h q", h=8)
            self._na_pv((pt, ptb, lc, first, last), po, ob)
            if not last:
                return
            qc = slice(i * 64, (i + 1) * 64)
            rc, rcb = self.next_tmp()
            rc3 = rc[:, :].rearrange("p (h q) -> p h q", h=8)
            self.act(rc3[64:128, 0:8:2, :], po[64:128, 0:8:2, :], AF.Ln, [self.Bk(ob)], [rcb])
            self.act(rc3[0:64, 1:8:2, :], po[0:64, 1:8:2, :], AF.Ln, [self.Bk(ob), rcb], [rcb])
            self.act(rc3[64:128, 0:8:2, :], rc3[64:128, 0:8:2, :], AF.Exp, [rcb], [rcb], scale=-1.0)
            self.act(rc3[0:64, 1:8:2, :], rc3[0:64, 1:8:2, :], AF.Exp, [rcb], [rcb], scale=-1.0)
            self.tt("dve", self.obT[0:64, :, qc], po[0:64, 0:8:2, :], rc3[64:128, 0:8:2, :], ALU.mult,
                    [self.Bk(ob), rcb], [Bf("obT")])
            self.tt("dve", self.obT[64:128, :, qc], po[64:128, 1:8:2, :], rc3[0:64, 1:8:2, :], ALU.mult,
                    [self.Bk(ob), rcb], [Bf("obT")])

        for t, (i, lc, m, mc, first, last) in enumerate(steps):
            qc = slice(i * 64, (i + 1) * 64)
            sp_ = 1 + (t % 3)
            pbufs = [self.Bk(2 * sp_), self.Bk(2 * sp_ + 1)]
            ps8 = self.PS[sp_][:, :].rearrange("p (h q) -> p h q", h=8)[:, :, 0:64]
            for a in range(2):
                for j in range(4):
                    self.mm(ps8[:, a * 4 + j, :], self.kbb[:, j, lc * 128:(lc + 1) * 128], self.qz[a][:, j, qc], True, True,
                            [Bf("kbb"), Bf("qz")], [self.Bk(2 * sp_ + a)])
            sc, scb = self.next_tmp()
            sc3 = sc[:, :].rearrange("p (h q) -> p h q", h=8)
            self.stt("dve", sc3, ps8, 0.125, self.wt[:, :, m, :], ALU.mult, ALU.add, pbufs + [Bf("wt")], [scb])
            pt, ptb = self.next_pt()
            self.act(pt[:, 0, :], sc[:, :], AF.Exp, [scb, Bf(U["maskbuf"])], [ptb], bias=U["mask"][:, mc:mc + 1])
            queue.append((i, lc, first, last, pt, ptb))
            if len(queue) > DEPTH:
                na_finish(queue.pop(0))
        while queue:
            na_finish(queue.pop(0))
        if self.debug and ti == 0:
            for nm, t in (("oaT", self.oaT), ("obT", self.obT)):
                dd = self.nc.dram_tensor("dbg_%s_%s" % (nm, u), [128, 4, 512], BF16, kind="ExternalOutput").ap()
                self.load(dd, t[:, :, :], [Bf(nm if not nm.startswith("qz") else "qz")], [Bf("dbg_" + nm)], group="st")
        self.trickle_late(4)
        if self.stop == "mix_na":
            return
        wpa4, wpab = self.wpa_t, Bf("wpa_t")
        wpb4, wpbb = self.wpb_t, Bf("wpb_t")
        for pn in range(4):
            wg, wgb = self.next_wr()
            self.load(wg[:, :, :], self.wg_s[pn], [Bf("wg_s")], [wgb], group="w")
            for f in range(2):
                fc = pn * 2 + f
                b0 = 4 * (fc % 2)
                for kc in range(8):
                    self.mm(self.bank(b0)[:, 0:NT], wg[:, kc, f * 128:(f + 1) * 128], self.hT[:, kc, 0:NT], kc == 0, kc == 7,
                            [wgb, Bf("hT")], [self.Bk(b0)])
                for kc in range(8):
                    self.mm(self.bank(b0 + 1)[:, 0:NT], wg[:, kc, 256 + f * 128:256 + (f + 1) * 128], self.hT[:, kc, 0:NT],
                            kc == 0, kc == 7, [wgb, Bf("hT")], [self.Bk(b0 + 1)])
                for kc in range(4):
                    self.mm(self.bank(b0 + 2)[:, 0:NT], wpa4[:, kc, fc * 128:(fc + 1) * 128], self.oaT[:, kc, 0:NT],
                            kc == 0, kc == 3, [wpab, Bf("oaT")], [self.Bk(b0 + 2)])
                for kc in range(4):
                    self.mm(self.bank(b0 + 3)[:, 0:NT], wpb4[:, kc, fc * 128:(fc + 1) * 128], self.obT[:, kc, 0:NT],
                            kc == 0, kc == 3, [wpbb, Bf("obT")], [self.Bk(b0 + 3)])
                sa, sab = self.next_tmp()
                self.act(sa[:, 0:NT], self.bank(b0)[:, 0:NT], AF.Sigmoid, [self.Bk(b0), Bf("bg")], [sab],
                         bias=self.bg[:, fc:fc + 1])
                sg, sgb = self.next_tmp()
                self.act(sg[:, 0:NT], self.bank(b0 + 1)[:, 0:NT], AF.Sigmoid, [self.Bk(b0 + 1), Bf("bg")], [sgb],
                         bias=self.bg[:, 8 + fc:9 + fc])
                self.tt("dve", sa[:, 0:NT], self.bank(b0 + 2)[:, 0:NT], sa[:, 0:NT], ALU.mult, [self.Bk(b0 + 2), sab], [sab])
                self.tt("dve", sg[:, 0:NT], self.bank(b0 + 3)[:, 0:NT], sg[:, 0:NT], ALU.mult, [self.Bk(b0 + 3), sgb], [sgb])
                self.tt("pool", self.mT[:, fc, 0:NT], sa[:, 0:NT], sg[:, 0:NT], ALU.add, [sab, sgb], [Bf("mT")])
        if self.debug and ti == 0:
            dd = self.nc.dram_tensor("dbg_mT_%s" % u, [128, 8, 512], BF16, kind="ExternalOutput").ap()
            self.load(dd, self.mT[:, :, :], [Bf("mT")], [Bf("dbg_mT")], group="st")
        self.trickle_late(4)
        if self.stop == "mix_merge":
            return
        wo = []
        for hf in range(2):
            w_, wb_ = self.next_wr()
            self.load(w_[:, :, :], self.wo_s[hf], [Bf("wo_s")], [wb_], group="w")
            wo.append((w_, wb_))
        self.memset("pool", self.ss[:, 0:ns], 0.0, [Bf("ss")])
        for s in range(ns):
            pp = s
            for hf in range(2):
                bk = pp * 2 + hf
                for kc in range(8):
                    self.mm(self.bank(bk)[:, :], self.mT[:, kc, s * 128:(s + 1) * 128], wo[hf][0][:, kc, :], kc == 0, kc == 7,
                            [Bf("mT"), wo[hf][1]], [self.Bk(bk)])
        for s in range(ns):
            pp = s
            mixp = self.PS[pp][:, :]
            pbufs = [self.Bk(pp * 2), self.Bk(pp * 2 + 1)]
            xn = self.xn[s % 2]
            self.act(xn[:, :], mixp, AF.Square, pbufs + [Bf("ss")], [Bf("xn%d" % (s % 2)), Bf("ss")],
                     accum=self.ss[:, s:s + 1])
            rsb = Bf("rstd_w%d" % s)
            self.act_rsqrt(self.rstd[:, s:s + 1], self.ss[:, s:s + 1], 1.0 / D, [Bf("ss")], [rsb])
            t0, t0b = self.next_tmp()
            t1, t1b = self.next_tmp()
            for hf, (tq, tqb) in enumerate(((t0, t0b), (t1, t1b))):
                self.stt("dve", tq[:, :], self.bank(pp * 2 + hf)[:, :], self.rstd[:, s:s + 1],
                         self.gpm[:, hf * 512:(hf + 1) * 512], ALU.mult, ALU.mult,
                         [self.Bk(pp * 2 + hf), rsb, Bf("gpm")], [tqb])
                self.tt("dve", self.xt[:, s, hf * 512:(hf + 1) * 512], self.xt[:, s, hf * 512:(hf + 1) * 512], tq[:, :],
                        ALU.add, [Bf("xt"), tqb], [Bf("xt")])
        if self.stop == "mix_wout":
            return
        own0 = U["own0"]
        for s in range(ns):
            ta = tl0 + s * 128
            lo = max(ta, own0 * 64)
            hi = min(ta + 128, own0 * 64 + 2048)
            if hi <= lo:
                continue
            yt = sorted(set([(lo - own0 * 64) // 512, (hi - 1 - own0 * 64) // 512]))
            self.load(U["y"][lo - own0 * 64:hi - own0 * 64, :], self.xt[lo - ta:hi - ta, s, :], [Bf("xt")],
                      [Bf("y_%s_%d" % (u, k)) for k in yt], group="st")
        if not defer_h2:
            self.mix_h2(u, U, ti, prefetch_next)

    def _gqa_pv(self, pend, NT):
        pts, vr, vrb, c, first, last = pend
        for j in range(4):
            pt, ptb = pts[j // 2]
            self.mm(self.bank(j)[:, 0:NT], vr[:, c, :], pt[:, j % 2, 0:NT], first, last, [vrb, ptb], [self.Bk(j)])

    def _na_pv(self, pend, po, ob):
        pt, ptb, lc, first, last = pend
        pt3 = pt[:, 0, :].rearrange("p (h q) -> p h q", h=8)
        for h in range(8):
            j, a = h // 2, h % 2
            c0 = j * 192 + a * 64
            self.mm(po[:, h, :], self.vbb[:, lc, c0:c0 + 128], pt3[:, a * 4 + j, :], first and h == 0, last,
                    [self.B("vbb"), ptb], [self.Bk(ob)], skip=True)

    def ffn_preload(self, u, U, fi):
        Bf = self.B
        cb = (U["own0"] - U["qrow0"]) * 64 + fi * 512
        self.load(self.hw[:, :, :], U["h2T"][:, :, cb:cb + 514], [Bf("h2T_" + u)], [Bf("hw")], group="x")

    def ffn_tile(self, u, U, fi, preloaded=False, between=None):
        Bf = self.B
        samp = U["samp"]
        cb = (U["own0"] - U["qrow0"]) * 64 + fi * 512
        self.trickle_late(1000)
        if not preloaded:
            self.load(self.hw[:, :, :], U["h2T"][:, :, cb:cb + 514], [Bf("h2T_" + u)], [Bf("hw")], group="x")
        if samp and fi == 0:
            self.ts("pool", self.hw[:, :, 0:1], self.hw[:, :, 0:1], self.flags[:, 0:1], None, ALU.mult, None,
                    [Bf("hw"), Bf("flags")], [Bf("hw")])
        if samp and fi == 3:
            self.ts("pool", self.hw[:, :, 513:514], self.hw[:, :, 513:514], self.flags[:, 1:2], None, ALU.mult, None,
                    [Bf("hw"), Bf("flags")], [Bf("hw")])
        hwh = self.hw[:, :, 0:514:513]
        for pn in range(11):
            wu, wub = self.next_wr()
            self.load(wu[:, :, :], self.wup_s[pn], [Bf("wup_s")], [wub], group="w")
            for f in range(2):
                fc = pn * 2 + f
                hb = 6 + (fc % 2)
                cw = self.conv
                info = []
                for half in range(2):
                    cid = fc + 22 * half
                    bk = (fc % 3) * 2 + half
                    wc = slice(half * 256 + f * 128, half * 256 + (f + 1) * 128)
                    for kc in range(8):
                        self.mm(self.bank(bk)[:, :], wu[:, kc, wc], self.hw[:, kc, 1:513], kc == 0, kc == 7,
                                [wub, Bf("hw")], [self.Bk(bk)])
                    uh = self.bank(hb)[:, half * 2:half * 2 + 2]
                    for kc in range(8):
                        self.mm(uh, wu[:, kc, wc], hwh[:, kc, :], kc == 0, kc == 7, [wub, Bf("hw")], [self.Bk(hb)])
                    c_, cb_ = self.next_tmp()
                    info.append((cid, bk, uh, c_, cb_))
                for (cid, bk, uh, c_, cb_) in info:
                    self.act(c_[:, :], self.bank(bk)[:, :], AF.Identity, [self.Bk(bk), Bf("conv")], [cb_],
                             bias=cw[:, cid, 3:4], scale=cw[:, cid, 1:2])
                for (cid, bk, uh, c_, cb_) in info:
                    self.stt("dve", c_[:, 1:512], self.bank(bk)[:, 0:511], cw[:, cid, 0:1], c_[:, 1:512], ALU.mult, ALU.add,
                             [self.Bk(bk), cb_, Bf("conv")], [cb_])
                for (cid, bk, uh, c_, cb_) in info:
                    self.stt("dve", c_[:, 0:511], self.bank(bk)[:, 1:512], cw[:, cid, 2:3], c_[:, 0:511], ALU.mult, ALU.add,
                             [self.Bk(bk), cb_, Bf("conv")], [cb_])
                for (cid, bk, uh, c_, cb_) in info:
                    self.stt("dve", c_[:, 0:1], uh[:, 0:1], cw[:, cid, 0:1], c_[:, 0:1], ALU.mult, ALU.add,
                             [self.Bk(hb), cb_, Bf("conv")], [cb_])
                for (cid, bk, uh, c_, cb_) in info:
                    self.stt("dve", c_[:, 511:512], uh[:, 1:2], cw[:, cid, 2:3], c_[:, 511:512], ALU.mult, ALU.add,
                             [self.Bk(hb), cb_, Bf("conv")], [cb_])
                (_, _, _, cg, cgb), (_, _, _, cv, cvb) = info
                self.act(cg[:, :], cg[:, :], AF.Gelu_apprx_tanh, [cgb], [cgb])
                self.tt("pool", self.gT[:, fc, :], cg[:, :], cv[:, :], ALU.mult, [cgb, cvb],
                        [Bf("gT"), Bf("kbb"), Bf("vbb")])
        self.memset("pool", self.ss[:, 0:4], 0.0, [Bf("ss")])
        dn_banks = ([2, 3, 4, 5], [6, 7, 2, 3])
        for ps_ in range(2):
            for q6 in range(6):
                k0 = q6 * 4
                nk = min(4, 22 - k0)
                wd, wdb = self.next_wr()
                wd2 = wd[:, :, :].rearrange("p a b -> p (a b)").rearrange("p (k n) -> p k n", k=4)
                self.load(wd2[:, 0:nk, :], self.wdn_s[:, k0:k0 + nk, :], [Bf("wdn_s")], [wdb], group="w")
                for si in range(2):
                    s = ps_ * 2 + si
                    for hf in range(2):
                        bk = dn_banks[ps_][si * 2 + hf]
                        for kk in range(nk):
                            fc = k0 + kk
                            self.mm(self.bank(bk)[:, :], self.gT[:, fc, s * 128:(s + 1) * 128],
                                    wd2[:, kk, hf * 512:(hf + 1) * 512], fc == 0, fc == 21,
                                    [Bf("gT"), wdb], [self.Bk(bk)])
            if ps_ == 0 and between is not None:
                between()
            for si in range(2):
                s = ps_ * 2 + si
                pi = dn_banks[ps_][si * 2] // 2
                fp = self.PS[pi][:, :]
                pbufs = [self.Bk(pi * 2), self.Bk(pi * 2 + 1)]
                xn = self.xn[s % 2]
                self.act(xn[:, :], fp, AF.Square, pbufs + [Bf("ss")], [Bf("xn%d" % (s % 2)), Bf("ss")],
                         accum=self.ss[:, s:s + 1])
                self.act_rsqrt(self.rstd[:, s:s + 1], self.ss[:, s:s + 1], 1.0 / D, [Bf("ss")], [Bf("rstd")])
                fo, fob = self.fo[s % 2], Bf("fo%d" % (s % 2))
                self.stt("dve", fo[:, :], fp, self.rstd[:, s:s + 1], self.gpo[:, :], ALU.mult, ALU.mult,
                         pbufs + [Bf("rstd"), Bf("gpo")], [fob])
                ybuf = Bf("y_%s_%d" % (u, fi))
                self.accum_store(U["y"][fi * 512 + s * 128:fi * 512 + (s + 1) * 128, :], fo[:, :], [fob, ybuf], [ybuf])


_CACHE = {}


def host_consts(inp):
    f32 = np.float32
    c = {}
    c["c_gpre"] = np.ascontiguousarray(inp["pre_mix_norm"].reshape(8, 128).T).astype(f32)
    c["c_gpf"] = np.ascontiguousarray(inp["pre_ffn_norm"].reshape(8, 128).T).astype(f32)
    c["c_gpm"] = np.ascontiguousarray(np.broadcast_to(inp["post_mix_norm"].reshape(1, D), (128, D))).astype(f32)
    c["c_gpo"] = np.ascontiguousarray(np.broadcast_to(inp["post_ffn_norm"].reshape(1, D), (128, D))).astype(f32)
    c["c_bg"] = np.ascontiguousarray(inp["b_gate"].reshape(16, 128).T).astype(f32)
    qw = np.tile(inp["q_norm"].reshape(64), 2)
    kw = np.tile(inp["k_norm"].reshape(64), 2)
    c["c_qkw"] = np.ascontiguousarray(np.stack([qw, kw], axis=1)).astype(f32)
    cw = inp["conv_w"].reshape(3, 2 * DFF)
    cb = inp["conv_b"].reshape(1, 2 * DFF)
    c4 = np.concatenate([cw, cb], axis=0)
    c["c_conv"] = np.ascontiguousarray(c4.reshape(4, 44, 128).transpose(2, 1, 0).reshape(128, 44 * 4)).astype(f32)
    rpb = inp["rpb"].reshape(8, 15, 31)
    a = (np.arange(128) // 64)[:, None, None, None]
    ck = (np.arange(128) % 64)[:, None, None, None]
    hp = np.arange(8)
    h = (2 * (hp % 4) + hp // 4)[None, :, None, None]
    m = np.arange(15)[None, None, :, None]
    cq = np.arange(64)[None, None, None, :]
    dr = np.minimum(m + a, 14)
    dc = np.clip(ck - cq, -15, 15) + 15
    c["c_wt"] = np.ascontiguousarray(rpb[h, dr, dc].reshape(128, 8 * 15 * 64)).astype(f32)
    return c


def static_consts():
    f32 = np.float32
    c = {}
    ckk = np.arange(64)[:, None]
    cqq = np.arange(64)[None, :]
    cs = np.clip(cqq - 8, 0, 48)
    inwin = (ckk >= cs) & (ckk < cs + 16)
    cm = np.where(inwin, 0.0, NEG).astype(f32)
    c["c_cm"] = np.concatenate([cm, cm], axis=0)
    c["c_mp"] = PROMPT_MASK.astype(f32)
    c["c_ident"] = np.eye(128).astype(ml_dtypes.bfloat16)
    bo = np.zeros((128, 128), f32)
    bo[0:64, 0:64] = 1
    bo[64:128, 64:128] = 1
    c["c_bones"] = bo
    lt = np.zeros((128, 128), f32)
    for d in range(128):
        if d % 32 < 16:
            lt[d + 16, d] = -1.0
        else:
            lt[d - 16, d] = 1.0
    c["c_rotm"] = lt
    ck_, sk_ = rope_tables(8192)
    c["c_cosk"] = ck_
    c["c_sink"] = sk_
    return c


def get_builder(units, debug, stop=None):
    key = (tuple(units), debug, stop)
    if key not in _CACHE:
        _CACHE[key] = Builder(units=units, debug=debug, stop=stop)
    return _CACHE[key]


def make_in_maps(inputs):
    inp = {k: np.asarray(v) for k, v in inputs.items()}
    xpr = inp["x_prompt"]
    xsa = inp["x_sample"]
    hc = host_consts(inp)
    sc = static_consts()
    wts = dict(
        w_in=np.ascontiguousarray(inp["w_in"].reshape(D, DIN)),
        w_pa=np.ascontiguousarray(inp["w_proj_a"].reshape(512, D)),
        w_pb=np.ascontiguousarray(inp["w_proj_b"].reshape(512, D)),
        w_out=np.ascontiguousarray(inp["w_out"].reshape(D, D)),
        w_up=np.ascontiguousarray(inp["w_up"].reshape(D, 2 * DFF)),
        w_dn=np.ascontiguousarray(inp["w_down"].reshape(DFF, D)),
    )
    cosk, sink = sc["c_cosk"], sc["c_sink"]
    in_maps = []
    for c in range(NCORES):
        sq, cq = c // 4, c % 4
        m = {}
        m["xp"] = np.ascontiguousarray(xpr[4 * c:4 * c + 4])
        m["xsf"] = np.ascontiguousarray(xsa[sq])
        r0 = 32 * cq
        loc = np.zeros((2816, D), np.float32)
        g0 = (r0 - 6) * 64
        lo, hi = max(g0, 0), min(g0 + 2816, 8192)
        loc[lo - g0:hi - g0] = xsa[sq, lo:hi]
        m["xsl"] = loc
        tq = (r0 - 1) * 64 + np.arange(2176)
        tq = np.clip(tq, 0, 8191)
        m["c_cosq"] = np.ascontiguousarray(cosk[:, tq])
        m["c_sinq"] = np.ascontiguousarray(sink[:, tq])
        m["c_ms"] = SAMPLE_MASKS[cq].astype(np.float32)
        fl = np.ones((128, 2), np.float32)
        if cq == 0:
            fl[:, 0] = 0.0
        if cq == 3:
            fl[:, 1] = 0.0
        m["c_flags"] = fl
        m.update(hc)
        m.update(sc)
        m.update(wts)
        in_maps.append(m)
    return in_maps


def run(inputs, units=("p0", "p1", "p2", "p3", "s"), debug=False, stop=None):
    b = get_builder(units, debug, stop)
    in_maps = make_in_maps(inputs)
    res = run_bass_kernel_spmd(b.nc, in_maps, core_ids=list(range(NCORES)))
    return res


def kernel(**inputs):
    res = run(inputs)
    yp = np.concatenate([np.asarray(r["yp"]) for r in res.results], axis=0).astype(np.float32)
    ys = np.stack([np.concatenate([np.asarray(res.results[s * 4 + q]["ys"]) for q in range(4)], axis=0)
                   for s in range(2)], axis=0).astype(np.float32)
    return (yp, ys)
```

```python
import contextlib
import numpy as np
import ml_dtypes
import concourse.bass as bass
import concourse.mybir as mybir
from concourse.bass_utils import run_bass_kernel_spmd

F32 = mybir.dt.float32
BF16 = mybir.dt.bfloat16
AF = mybir.ActivationFunctionType
ALU = mybir.AluOpType

D = 1024
DIN = 4352
DFF = 2816
EPS = 1e-6
NEG = -30000.0
NCORES = 8


class Buf:
    __slots__ = ("name", "writers", "readers")

    def __init__(self, name):
        self.name = name
        self.writers = {}
        self.readers = {}


class Op:
    __slots__ = ("q", "key", "fn", "deps", "signal", "ordinal", "is_dma")

    def __init__(self, q, key, fn, is_dma):
        self.q = q
        self.key = key
        self.fn = fn
        self.deps = []
        self.signal = is_dma
        self.ordinal = None
        self.is_dma = is_dma


ENGINES = ("pe", "act", "dve", "pool", "sp")


class Sched:
    def __init__(self, nc, n_chan=8):
        self.nc = nc
        self.ops = {e: [] for e in ENGINES}
        self.n_chan = n_chan
        self.chan_last = {}
        self.chan_rr = {}
        self.last_by_key = {}
        self.pending_barrier = {e: None for e in ENGINES}

    def barrier(self):
        snap = dict(self.last_by_key)
        for e in ENGINES:
            self.pending_barrier[e] = snap

    def _add(self, op, reads, writes):
        deps = {}
        for b in reads:
            for w in b.writers.values():
                deps[id(w)] = (w, "raw")
        for b in writes:
            for w in b.writers.values():
                deps[id(w)] = (w, "waw")
            for r in b.readers.values():
                if id(r) not in deps:
                    deps[id(r)] = (r, "war")
        pb = self.pending_barrier[op.q]
        if pb is not None:
            for d in pb.values():
                if id(d) not in deps:
                    deps[id(d)] = (d, "raw")
            self.pending_barrier[op.q] = None
        for d, typ in deps.values():
            if d is op:
                continue
            if (not op.is_dma) and (not d.is_dma) and d.key == op.key:
                if op.key == "pe":
                    continue
                if typ == "war":
                    continue
            op.deps.append(d)
            d.signal = True
        for b in reads:
            b.readers[op.key] = op
        for b in writes:
            b.writers = {op.key: op}
            b.readers = {}
        self.ops[op.q].append(op)
        self.last_by_key[op.key] = op
        return op

    def op(self, eng, fn, reads=(), writes=()):
        return self._add(Op(eng, eng, fn, False), reads, writes)

    def dma(self, q, fn, reads=(), writes=(), group="g"):
        rr = self.chan_rr.get((q, group), 0)
        self.chan_rr[(q, group)] = rr + 1
        key = "dma_%s_%s_%d" % (q, group, rr % self.n_chan)
        o = Op(q, key, fn, True)
        prev = self.chan_last.get(key)
        if prev is not None:
            o.deps.append(prev)
        self.chan_last[key] = o
        return self._add(o, reads, writes)

    def emit(self):
        nc = self.nc
        counts = {}
        for e in ENGINES:
            for o in self.ops[e]:
                if o.signal:
                    inc = 16 if o.is_dma else 1
                    counts[o.key] = counts.get(o.key, 0) + inc
                    o.ordinal = counts[o.key]
        keys = sorted(counts.keys())
        sems = {}
        with contextlib.ExitStack() as st:
            for k in keys:
                sems[k] = st.enter_context(nc.semaphore("s_" + k))
            block = st.enter_context(nc.Block())
            engmap = {"pe": block.tensor, "act": block.scalar, "dve": block.vector,
                      "pool": block.gpsimd, "sp": block.sync}
            final_waits = {k: counts[k] for k in keys if k.startswith("dma_")}

            def make(ename):
                ops = self.ops[ename]

                def body(eng):
                    waited = {}
                    for o in ops:
                        need = {}
                        for d in o.deps:
                            v = d.ordinal
                            if waited.get(d.key, 0) >= v:
                                continue
                            if need.get(d.key, 0) < v:
                                need[d.key] = v
                        for k, v in need.items():
                            eng.wait_ge(sems[k], v)
                            waited[k] = v
                        ins = o.fn(eng)
                        if o.signal:
                            ins.then_inc(sems[o.key], 16 if o.is_dma else 1)
                    if ename == "sp":
                        for k, v in final_waits.items():
                            if waited.get(k, 0) < v:
                                eng.wait_ge(sems[k], v)
                return body

            for e in ENGINES:
                if self.ops[e] or e == "sp":
                    engmap[e](make(e))
        return counts


def prompt_na_geometry():
    rows = []
    cols = []
    for r in range(32):
        start = min(max(r - 4, 0), 24)
        c0, c1 = start // 2, (start + 7) // 2
        lst = []
        for ci in range(c0, c1 + 1):
            m = 2 * ci - r + 7
            assert 0 <= m <= 13
            col = np.zeros(128, np.float32)
            for a in range(2):
                ok = start <= 2 * ci + a < start + 8
                if not ok:
                    col[a * 64:(a + 1) * 64] = NEG
                else:
                    assert m + a <= 14
            lst.append((ci, m, len(cols)))
            cols.append(col)
        rows.append(lst)
    return rows, np.stack(cols, axis=1)


def sample_na_geometry():
    def band(c, l):
        R = 32 * c - 6 + l
        if 0 <= R < 128:
            sg = min(max(R - 4, 0), 120)
        else:
            sg = R - 4
        sl = sg - 32 * c + 6
        return sl
    rows = []
    ncol = 0
    per_core_cols = [[] for _ in range(4)]
    for l in range(5, 39):
        starts = [band(c, l) for c in range(4)]
        lo = min(starts)
        hi = max(starts) + 8
        c0, c1 = lo // 2, (hi - 1) // 2
        lst = []
        for ci in range(c0, c1 + 1):
            m = 2 * ci - l + 7
            assert 0 <= m <= 13, (l, ci, m)
            assert 0 <= ci < 22
            for c in range(4):
                col = np.zeros(128, np.float32)
                for a in range(2):
                    ok = starts[c] <= 2 * ci + a < starts[c] + 8
                    if not ok:
                        col[a * 64:(a + 1) * 64] = NEG
                    else:
                        assert m + a <= 14
                per_core_cols[c].append(col)
            lst.append((ci, m, ncol))
            ncol += 1
        rows.append(lst)
    masks = [np.stack(cc, axis=1) for cc in per_core_cols]
    return rows, masks


def rope_tables(npos):
    t = np.arange(npos)
    row = (t // 64).astype(np.float32)
    col = (t % 64).astype(np.float32)
    half = 32
    freqs = (10000.0 ** (-np.arange(0, half, 2, dtype=np.float32) / half)).astype(np.float32)
    ang_r = row[:, None] * freqs[None, :]
    ang_c = col[:, None] * freqs[None, :]
    ang = np.concatenate([ang_r, ang_r, ang_c, ang_c], axis=-1).astype(np.float32)
    cos = np.cos(ang).astype(np.float32).T
    sin = np.sin(ang).astype(np.float32).T
    return np.concatenate([cos, cos], 0), np.concatenate([sin, sin], 0)


PROMPT_ROWS, PROMPT_MASK = prompt_na_geometry()
SAMPLE_ROWS, SAMPLE_MASKS = sample_na_geometry()
NMP = PROMPT_MASK.shape[1]
NMS = SAMPLE_MASKS[0].shape[1]


class Builder:
    def __init__(self, units=("p0", "p1", "p2", "p3", "s"), debug=False, stop=None):
        self.units = units
        self.debug = debug
        self.stop = stop
        self.nc = bass.Bass("TRN2", target_bir_lowering=False)
        self.S = Sched(self.nc)
        self.sb_off = 16512
        self.bufs = {}
        self.build()

    def sb(self, name, shape, dtype, off=None):
        esz = 4 if dtype == F32 else 2
        n = esz
        for s in shape[1:]:
            n *= s
        n = (n + 31) // 32 * 32
        if off is None:
            off = self.sb_off
            self.sb_off += n
            assert self.sb_off <= 229376, ("SBUF overflow", name, self.sb_off)
        t = self.nc.alloc_sbuf_tensor_at(name, list(shape), dtype, offset=off)
        return t

    def B(self, name):
        b = self.bufs.get(name)
        if b is None:
            b = Buf(name)
            self.bufs[name] = b
        return b

    def dram(self, name, shape, dtype, kind="Internal"):
        if self.debug and kind == "Internal":
            kind = "ExternalOutput"
        return self.nc.dram_tensor(name, list(shape), dtype, kind=kind).ap()

    def mm(self, out, lhsT, rhs, start, stop, rd, wr, skip=False):
        if skip:
            self.S.op("pe", lambda e: e.matmul(out, lhsT, rhs, start=start, stop=stop, skip_group_check=True), rd, wr)
        else:
            self.S.op("pe", lambda e: e.matmul(out, lhsT, rhs, start=start, stop=stop), rd, wr)

    def tr(self, out, in_, rd, wr):
        idn = self.ident[:, :]
        self.S.op("pe", lambda e: e.transpose(out, in_, idn), list(rd) + [self.B("ident")], wr)

    def act(self, out, in_, func, rd, wr, bias=None, scale=None, accum=None):
        kw = {}
        if bias is not None:
            kw["bias"] = bias
        if scale is not None:
            kw["scale"] = scale
        if accum is not None:
            kw["accum_out"] = accum
        self.S.op("act", lambda e: e.activation(out=out, in_=in_, func=func, **kw), rd, wr)

    def tt(self, eng, out, in0, in1, op, rd, wr):
        self.S.op(eng, lambda e: e.tensor_tensor(out, in0, in1, op), rd, wr)

    def ts(self, eng, out, in0, s1, s2, op0, op1, rd, wr):
        if s2 is None:
            self.S.op(eng, lambda e: e.tensor_scalar(out, in0, s1, None, op0), rd, wr)
        else:
            self.S.op(eng, lambda e: e.tensor_scalar(out, in0, s1, s2, op0, op1), rd, wr)

    def stt(self, eng, out, in0, scalar, in1, op0, op1, rd, wr):
        self.S.op(eng, lambda e: e.scalar_tensor_tensor(out, in0, scalar, in1, op0, op1), rd, wr)

    def act_rsqrt(self, out, in_, scale, rd, wr):
        self.act(out, in_, AF.Ln, list(rd) + [self.B("epsc")], wr, bias=self.epsc[:, 0:1], scale=scale)
        self.act(out, out, AF.Exp, wr, wr, scale=-0.5)

    def act_recip(self, out, in_, rd, wr):
        self.act(out, in_, AF.Ln, rd, wr)
        self.act(out, out, AF.Exp, wr, wr, scale=-1.0)

    def recip(self, out, in_, rd, wr):
        self.S.op("dve", lambda e: e.reciprocal(out, in_), rd, wr)

    def cp(self, eng, out, in_, rd, wr):
        if eng == "act":
            self.act(out, in_, AF.Copy, rd, wr)
        else:
            self.S.op(eng, lambda e: e.tensor_copy(out, in_), rd, wr)

    def memset(self, eng, ap, val, wr):
        self.S.op(eng, lambda e: e.memset(ap, val), (), wr)

    def load(self, out, in_, rd, wr, q="sp", group="g", slow=False):
        if slow:
            self.S.dma(q, lambda e: e.dma_start(out=out, in_=in_, allow_slow_non_contiguous=True), rd, wr, group=group)
        else:
            self.S.dma(q, lambda e: e.dma_start(out=out, in_=in_), rd, wr, group=group)

    def accum_store(self, out, in_, rd, wr):
        self.S.dma("pool", lambda e: e.dma_start(out=out, in_=in_, accum_op=ALU.add), rd, wr, group="acc")

    def bank(self, b):
        return self.PS[b // 2][:, (b % 2) * 512:(b % 2) * 512 + 512]

    def Bk(self, b):
        return self.B("bank%d" % b)

    def build(self):
        nc = self.nc
        din = lambda n, s, d=F32: nc.dram_tensor(n, list(s), d, kind="ExternalInput").ap()
        self.xp = din("xp", [4, 2048, D])
        self.xsf = din("xsf", [8192, D])
        self.xsl = din("xsl", [2816, D])
        self.yp = nc.dram_tensor("yp", [4, 2048, D], F32, kind="ExternalOutput").ap()
        self.ys = nc.dram_tensor("ys", [2048, D], F32, kind="ExternalOutput").ap()
        w_in = din("w_in", [D, DIN])
        w_pa = din("w_pa", [512, D])
        w_pb = din("w_pb", [512, D])
        w_out = din("w_out", [D, D])
        w_up = din("w_up", [D, 2 * DFF])
        w_dn = din("w_dn", [DFF, D])
        c_gpre = din("c_gpre", [128, 8])
        c_gpf = din("c_gpf", [128, 8])
        c_gpm = din("c_gpm", [128, D])
        c_gpo = din("c_gpo", [128, D])
        c_bg = din("c_bg", [128, 16])
        c_qkw = din("c_qkw", [128, 2])
        c_conv = din("c_conv", [128, 44 * 4])
        c_wt = din("c_wt", [128, 8 * 15 * 64])
        c_cm = din("c_cm", [128, 64])
        c_mp = din("c_mp", [128, NMP])
        c_ms = din("c_ms", [128, NMS])
        c_flags = din("c_flags", [128, 2])
        c_ident = din("c_ident", [128, 128], BF16)
        c_bones = din("c_bones", [128, 128])
        c_rotm = din("c_rotm", [128, 128])
        self.ropeK = (din("c_cosk", [128, 8192]), din("c_sink", [128, 8192]))
        self.ropeQs = (din("c_cosq", [128, 2176]), din("c_sinq", [128, 2176]))

        self.wqa_s = self.dram("wqa_s", [128, 8, 512], BF16)
        self.wqb_s = self.dram("wqb_s", [128, 8, 512], BF16)
        self.wk_s = self.dram("wk_s", [128, 8, 1280], BF16)
        self.wg_s = self.dram("wg_s", [4, 128, 8, 512], BF16)
        self.wpa_s = self.dram("wpa_s", [128, 4, 1024], BF16)
        self.wpb_s = self.dram("wpb_s", [128, 4, 1024], BF16)
        self.wo_s = self.dram("wo_s", [2, 128, 8, 512], BF16)
        self.wup_s = self.dram("wup_s", [11, 128, 8, 512], BF16)
        self.wdn_s = self.dram("wdn_s", [128, 22, 1024], BF16)

        S = self.S
        Bf = self.B
        self.ident = self.sb("ident", [128, 128], BF16)
        self.bones = self.sb("bones", [128, 128], F32)
        self.rotm = self.sb("rotm", [128, 128], F32)
        self.gpre = self.sb("gpre", [128, 8], F32)
        self.gpf = self.sb("gpf", [128, 8], F32)
        self.gpm = self.sb("gpm", [128, D], F32)
        self.gpo = self.sb("gpo", [128, D], F32)
        self.bg = self.sb("bg", [128, 16], F32)
        self.qkw = self.sb("qkw", [128, 2], F32)
        self.conv = self.sb("conv", [128, 44, 4], F32)
        self.wt = self.sb("wt", [128, 8, 15, 64], BF16)
        self.cm = self.sb("cm", [128, 64], F32)
        self.mp = self.sb("mp", [128, NMP], F32)
        self.ms = self.sb("ms", [128, NMS], F32)
        self.flags = self.sb("flags", [128, 2], F32)
        self.epsc = self.sb("epsc", [128, 1], F32)
        self.zero_bf = self.sb("zero_bf", [128, 8], BF16)
        cl = [(self.ident, c_ident, "ident"), (self.bones, c_bones, "bones"), (self.rotm, c_rotm, "rotm"),
              (self.gpre, c_gpre, "gpre"), (self.gpf, c_gpf, "gpf"), (self.gpm, c_gpm, "gpm"),
              (self.gpo, c_gpo, "gpo"), (self.bg, c_bg, "bg"), (self.qkw, c_qkw, "qkw"),
              (self.cm, c_cm, "cm"), (self.mp, c_mp, "mp"), (self.ms, c_ms, "ms"), (self.flags, c_flags, "flags")]
        for t, src, nm in cl:
            self.load(t[:, :], src, (), [Bf(nm)])
        self.load(self.conv[:, :, :], c_conv.rearrange("p (c k) -> p c k", k=4), (), [Bf("conv")])
        self.load(self.wt[:, :, :, :], c_wt.rearrange("p (h m q) -> p h m q", h=8, m=15), (), [Bf("wt")], q="pool", group="cv")
        self.memset("pool", self.epsc[:, :], EPS, [Bf("epsc")])
        self.memset("pool", self.zero_bf[:, :], 0.0, [Bf("zero_bf")])
        wt3 = self.wt[:, :, :, :].rearrange("p h m q -> p (h m) q")
        for i in range(4):
            sl = slice(i * 30, (i + 1) * 30)
            self.tt("dve", wt3[:, sl, :], wt3[:, sl, :], self.cm[:, :].unsqueeze(1).broadcast_to([128, 30, 64]),
                    ALU.add, [Bf("wt"), Bf("cm")], [Bf("wt")])

        self.pending_conv = []
        self.pending_late = []
        self.resident_loaded = False

        def conv_now(dst, src, nm):
            S.dma("pool", lambda e: e.dma_start(out=dst, in_=src), (), [Bf(nm)], group="cv")

        def conv_dma(dst, src, nm):
            if nm == "wk_s":
                conv_now(dst, src, nm)
            elif nm in ("wup_s", "wdn_s"):
                self.pending_late.append(lambda: conv_now(dst, src, nm))
            else:
                self.pending_conv.append(lambda: conv_now(dst, src, nm))

        conv_dma(self.wk_s[:, :, 0:256], w_in[:, 512:768].rearrange("(k p) n -> p k n", p=128), "wk_s")
        conv_dma(self.wk_s[:, :, 256:768], w_in[:, 1280:1792].rearrange("(k p) n -> p k n", p=128), "wk_s")
        conv_dma(self.wk_s[:, :, 768:1280], w_in[:, 1792:2304].rearrange("(k p) n -> p k n", p=128), "wk_s")
        for j in range(4):
            for g in range(2):
                h = g * 4 + j
                conv_dma(self.wqa_s[:, :, j * 128 + g * 64:j * 128 + g * 64 + 64],
                         w_in[:, h * 64:(h + 1) * 64].rearrange("(k p) n -> p k n", p=128), "wqa_s")
        conv_dma(self.wqb_s, w_in[:, 768:1280].rearrange("(k p) n -> p k n", p=128), "wqb_s")
        for pn in range(4):
            conv_dma(self.wg_s[pn, :, :, 0:256], w_in[:, 2304 + pn * 256:2304 + (pn + 1) * 256].rearrange("(k p) n -> p k n", p=128), "wg_s")
            conv_dma(self.wg_s[pn, :, :, 256:512], w_in[:, 3328 + pn * 256:3328 + (pn + 1) * 256].rearrange("(k p) n -> p k n", p=128), "wg_s")
        for g in range(2):
            conv_dma(self.wpa_s[g * 64:(g + 1) * 64, :, :],
                     w_pa[g * 256:(g + 1) * 256, :].rearrange("(j p) n -> p j n", p=64), "wpa_s")
        conv_dma(self.wpb_s, w_pb.rearrange("(k p) n -> p k n", p=128), "wpb_s")
        for hf in range(2):
            conv_dma(self.wo_s[hf], w_out[:, hf * 512:(hf + 1) * 512].rearrange("(k p) n -> p k n", p=128), "wo_s")
        for pn in range(11):
            conv_dma(self.wup_s[pn, :, :, 0:256], w_up[:, pn * 256:(pn + 1) * 256].rearrange("(k p) n -> p k n", p=128), "wup_s")
            conv_dma(self.wup_s[pn, :, :, 256:512], w_up[:, DFF + pn * 256:DFF + (pn + 1) * 256].rearrange("(k p) n -> p k n", p=128), "wup_s")
        for q4 in range(2):
            conv_dma(self.wdn_s[:, q4 * 11:(q4 + 1) * 11, :],
                     w_dn[q4 * 1408:(q4 + 1) * 1408, :].rearrange("(k p) n -> p k n", p=128), "wdn_s")

        self.PS = [nc.alloc_psum_tensor("ps%d" % i, [128, 1024], F32) for i in range(4)]

        self.WR = [self.sb("wr%d" % i, [128, 8, 512], BF16) for i in range(4)]
        self.wpa_t = self.sb("wpa_t", [128, 4, 1024], BF16)
        self.wpb_t = self.sb("wpb_t", [128, 4, 1024], BF16)

        self.wr_i = 0
        self.xt = self.sb("xt", [128, 4, D], F32)
        self.xn = [self.sb("xn%d" % i, [128, D], BF16) for i in range(2)]
        self.hT = self.sb("hT", [128, 8, 512], BF16)
        self.ss = self.sb("ss", [128, 8], F32)
        self.rstd = self.sb("rstd", [128, 8], F32)
        self.tmp = [self.sb("tmp%d" % i, [128, 512], F32) for i in range(6)]
        self.tmp_i = 0
        self.ropec = self.sb("ropec", [128, 512], F32)
        self.ropes = self.sb("ropes", [128, 512], F32)
        base = self.sb_off
        self.qaz = [self.sb("qaz%d" % g, [128, 4, 512], BF16) for g in range(2)]
        self.qz = [self.sb("qz%d" % a, [128, 4, 512], BF16) for a in range(2)]
        self.oaT = self.sb("oaT", [128, 4, 512], BF16)
        self.obT = self.sb("obT", [128, 4, 512], BF16)
        self.mT = self.sb("mT", [128, 8, 512], BF16)
        self.PT = [self.sb("pt%d" % i, [128, 2, 512], BF16) for i in range(4)]
        self.pt_i = 0
        self.kring = [self.sb("kring%d" % i, [128, 1024], BF16) for i in range(2)]
        self.vring = [self.sb("vring%d" % i, [128, 8, 128], BF16) for i in range(2)]
        self.kv_i = 0
        kbb_off = self.sb_off
        self.kbb = self.sb("kbb", [128, 4, 9 * 128], BF16)
        self.vbb = self.sb("vbb", [128, 9, 768], BF16)
        mix_end = self.sb_off
        self.sb_off = base
        self.wk = self.sb("wk", [128, 8, 1280], BF16)
        self.kat = self.sb("kat", [128, 512], BF16)
        self.vat = self.sb("vat", [128, 4, 192], BF16)
        self.kbt = self.sb("kbt", [128, 4, 512], BF16)
        self.vbt = self.sb("vbt", [128, 4, 768], BF16)
        self.xt_b = self.sb("xt_b", [128, 4, D], F32)
        self.hT_b = self.sb("hT_b", [128, 8, 512], BF16)
        pre_end = self.sb_off
        self.sb_off = max(mix_end, pre_end)
        self.hw = self.sb("hw", [128, 8, 514], BF16)
        self.fo = [self.sb("fo%d" % i, [128, D], F32) for i in range(2)]
        self.gT = self.sb("gT", [128, 22, 512], BF16, off=kbb_off)
        assert 4 * 9 * 128 * 2 + 9 * 768 * 2 >= 22 * 512 * 2
        print("SBUF used:", self.sb_off, "of 229376")

        self.unit = {}
        for u in self.units:
            samp = (u == "s")
            SA = 8192 if samp else 2048
            RL = 44 if samp else 32
            NQ = 2176 if samp else 2048
            d = dict(
                samp=samp, SA=SA, RL=RL, NQ=NQ,
                kaT=self.dram("kaT_" + u, [128, SA], BF16),
                va=self.dram("va_" + u, [128, SA // 128, 192], BF16),
                kbT=self.dram("kbT_" + u, [128, 4, RL * 64], BF16),
                vb=self.dram("vb_" + u, [128, RL // 2, 768], BF16),
                h2T=self.dram("h2T_" + u, [128, 8, NQ + 2], BF16),
            )
            if samp:
                d.update(xA=self.xsf, xL=self.xsl, y=self.ys, qrow0=5, own0=6, rows=SAMPLE_ROWS, mask=self.ms, maskbuf="ms",
                         ropeQ=self.ropeQs, mix_tiles=[(5, 8), (13, 8), (21, 8), (29, 8), (37, 2)])
            else:
                i = int(u[1])
                d.update(xA=self.xp[i], xL=self.xp[i], y=self.yp[i], qrow0=0, own0=0, rows=PROMPT_ROWS, mask=self.mp, maskbuf="mp",
                         ropeQ=self.ropeK, mix_tiles=[(0, 8), (8, 8), (16, 8), (24, 8)])
            self.unit[u] = d

        for ui, u in enumerate(self.units):
            U = self.unit[u]
            if self.stop == "conv":
                break
            if ui > 0:
                S.barrier()
            self.prepass(u, U)
            S.barrier()
            if self.stop == "pre":
                break
            self.memset("pool", self.qz[0][64:128, :, :], 0.0, [Bf("qz")])
            self.memset("pool", self.qz[1][0:64, :, :], 0.0, [Bf("qz")])
            self.memset("pool", self.qaz[0][64:128, :, :], 0.0, [Bf("qaz")])
            self.memset("pool", self.qaz[1][0:64, :, :], 0.0, [Bf("qaz")])
            for colx in (0, U["NQ"] + 1):
                self.load(U["h2T"][:, :, colx:colx + 1], self.zero_bf[:, :].unsqueeze(2), [Bf("zero_bf")], [Bf("h2T_" + u)], slow=True)
            nt = len(U["mix_tiles"])
            if self.stop is not None and self.stop.startswith("mix"):
                self.mix_tile(u, U, 0)
                break
            if self.stop == "ffn0":
                self.mix_tile(u, U, 0, prefetch_next=True)
                self.mix_tile(u, U, 1, preloaded=True)
                self.ffn_tile(u, U, 0)
                break
            front_done = False
            for ti in range(nt):
                self.mix_tile(u, U, ti, preloaded=(ti > 0), prefetch_next=(ti + 1 < nt), front_done=front_done)
                front_done = False
                if ti >= 1 and ti - 1 < 4:
                    if ti + 1 < nt:
                        def btw(ti=ti):
                            self.mix_tile(u, U, ti + 1, preloaded=True, only_front=True)
                        self.ffn_tile(u, U, ti - 1, preloaded=not U["samp"], between=btw)
                        front_done = True
                    else:
                        self.ffn_tile(u, U, ti - 1, preloaded=not U["samp"])
            for fi in range(max(nt - 1, 0), 4):
                self.ffn_tile(u, U, fi)
        self.counts = S.emit()

    def trickle_late(self, n):
        for _ in range(n):
            if self.pending_late:
                self.pending_late.pop(0)()

    def next_tmp(self):
        i = self.tmp_i % len(self.tmp)
        self.tmp_i += 1
        return self.tmp[i], self.B("tmp%d" % i)

    def next_wr(self):
        i = self.wr_i % len(self.WR)
        self.wr_i += 1
        return self.WR[i], self.B("wr%d" % i)

    def next_pt(self):
        i = self.pt_i % len(self.PT)
        self.pt_i += 1
        return self.PT[i], self.B("pt%d" % i)

    def norm_transpose(self, ns, gain, gain_buf, out, out_buf, out_col0=0, xt=None, xtn="xt"):
        Bf = self.B
        if xt is None:
            xt = self.xt
        self.memset("pool", self.ss[:, 0:ns], 0.0, [Bf("ss")])
        for s in range(ns):
            xn = self.xn[s % 2]
            self.act(xn[:, :], xt[:, s, :], AF.Square, [Bf(xtn), Bf("ss")], [Bf("xn%d" % (s % 2)), Bf("ss")],
                     accum=self.ss[:, s:s + 1])
        self.act_rsqrt(self.rstd[:, 0:ns], self.ss[:, 0:ns], 1.0 / D, [Bf("ss")], [Bf("rstd")])
        for s in range(ns):
            xn = self.xn[s % 2]
            xb = Bf("xn%d" % (s % 2))
            if s % 2 == 0:
                self.act(xn[:, :], xt[:, s, :], AF.Identity, [Bf(xtn), Bf("rstd")], [xb], scale=self.rstd[:, s:s + 1])
            else:
                self.ts("dve", xn[:, :], xt[:, s, :], self.rstd[:, s:s + 1], None, ALU.mult, None,
                        [Bf(xtn), Bf("rstd")], [xb])
            bk = s % 2
            pb = self.bank(bk).bitcast(BF16).rearrange("p (k t) -> p k t", k=8)
            for kc in range(8):
                self.tr(pb[:, kc, :], xn[:, kc * 128:(kc + 1) * 128], [xb], [self.Bk(bk)])
            c0 = out_col0 + s * 128
            self.tt("dve", out[:, :, c0:c0 + 128], pb[:, :, :], gain[:, :].unsqueeze(2).broadcast_to([128, 8, 128]),
                    ALU.mult, [self.Bk(bk), gain_buf], [out_buf])

    def load_x(self, src, ns, xt=None, xtn="xt"):
        if xt is None:
            xt = self.xt
        self.load(xt[:, 0:ns, :], src.rearrange("(s p) d -> p s d", p=128), (), [self.B(xtn)], group="x")

    def qk_norm_rope(self, ps_bank, NT, wcol, ropec, ropes, outs, out_buf, rope_bufs):
        Bf = self.B
        ps = self.bank(ps_bank)[:, 0:NT]
        sq, sqb = self.next_tmp()
        self.act(sq[:, 0:NT], ps, AF.Square, [self.Bk(ps_bank)], [sqb])
        self.mm(self.bank(4)[:, 0:NT], self.bones[:, :], sq[:, 0:NT], True, True, [sqb, Bf("bones")], [self.Bk(4)])
        rt, rtb = self.next_tmp()
        self.act_rsqrt(rt[:, 0:NT], self.bank(4)[:, 0:NT], 1.0 / 64, [self.Bk(4)], [rtb])
        qn, qnb = self.next_tmp()
        self.stt("dve", qn[:, 0:NT], ps, self.qkw[:, wcol:wcol + 1], rt[:, 0:NT], ALU.mult, ALU.mult,
                 [self.Bk(ps_bank), rtb, Bf("qkw")], [qnb])
        self.mm(self.bank(5)[:, 0:NT], self.rotm[:, :], qn[:, 0:NT], True, True, [qnb, Bf("rotm")], [self.Bk(5)])
        t1, t1b = self.next_tmp()
        self.tt("pool", t1[:, 0:NT], qn[:, 0:NT], ropec, ALU.mult, [qnb] + rope_bufs, [t1b])
        t2, t2b = self.next_tmp()
        self.tt("dve", t2[:, 0:NT], self.bank(5)[:, 0:NT], ropes, ALU.mult, [self.Bk(5)] + rope_bufs, [t2b])
        for (psl, oap) in outs:
            self.tt("pool", oap, t1[psl, 0:NT], t2[psl, 0:NT], ALU.add, [t1b, t2b], [out_buf])

    def prepass(self, u, U):
        Bf = self.B
        samp = U["samp"]
        self.load(self.wk[:, :, :], self.wk_s, [Bf("wk_s")], [Bf("wk")])
        self.memset("pool", self.vat[:, :, 64:128], 1.0, [Bf("vat")])
        vbt5 = self.vbt[:, :, :].rearrange("p s (j c) -> p s j c", j=4)
        self.memset("pool", vbt5[:, :, :, 64:128], 1.0, [Bf("vbt")])
        tiles = []
        if samp:
            for i in range(16):
                tiles.append((U["xA"][i * 512:(i + 1) * 512, :], 4, True, i * 512, False, 0))
            for i in range(5):
                tiles.append((U["xL"][i * 512:(i + 1) * 512, :], 4, False, 0, True, i * 512))
            tiles.append((U["xL"][2560:2816, :], 2, False, 0, True, 2560))
        else:
            for i in range(4):
                tiles.append((U["xA"][i * 512:(i + 1) * 512, :], 4, True, i * 512, True, i * 512))
        xbufs = [(self.xt, "xt", self.hT, "hT"), (self.xt_b, "xt_b", self.hT_b, "hT_b")]
        self.load_x(tiles[0][0], tiles[0][1], xbufs[0][0], xbufs[0][1])
        def stageA(tix):
            (src, ns, doA, tA, doB, tB) = tiles[tix]
            xt_c, xtn_c, hT_c, hTn_c = xbufs[tix % 2]
            if tix + 1 < len(tiles):
                nb_ = xbufs[(tix + 1) % 2]
                self.load_x(tiles[tix + 1][0], tiles[tix + 1][1], nb_[0], nb_[1])
            self.norm_transpose(ns, self.gpre, Bf("gpre"), hT_c, Bf(hTn_c), xt=xt_c, xtn=xtn_c)

        def stageB(tix):
            (src, ns, doA, tA, doB, tB) = tiles[tix]
            NT = ns * 128
            xt_c, xtn_c, hT_c, hTn_c = xbufs[tix % 2]
            if doA:
                self.load(self.ropec[:, 0:NT], self.ropeK[0][:, tA:tA + NT], (), [Bf("ropec")], group="x")
                self.load(self.ropes[:, 0:NT], self.ropeK[1][:, tA:tA + NT], (), [Bf("ropes")], group="x")
                for kc in range(8):
                    self.mm(self.bank(2)[:, 0:NT], self.wk[:, kc, 0:128], hT_c[:, kc, 0:NT], kc == 0, kc == 7,
                            [Bf("wk"), Bf(hTn_c)], [self.Bk(2)])
                self.qk_norm_rope(2, NT, 1, self.ropec[:, 0:NT], self.ropes[:, 0:NT],
                                  [(slice(0, 128), self.kat[:, 0:NT])], Bf("kat"), [Bf("ropec"), Bf("ropes")])
                self.load(U["kaT"][:, tA:tA + NT], self.kat[:, 0:NT], [Bf("kat")], [Bf("kaT_" + u)], group="st")
                pv = self.bank(3).rearrange("p (s c) -> p s c", s=4)
                for s in range(ns):
                    for kc in range(8):
                        self.mm(pv[:, s, :], hT_c[:, kc, s * 128:(s + 1) * 128], self.wk[:, kc, 128:256],
                                kc == 0, kc == 7, [Bf("wk"), Bf(hTn_c)], [self.Bk(3)])
                vat4 = self.vat[:, :, :].rearrange("p s (a c) -> p s a c", a=3)
                pv4 = self.bank(3).rearrange("p (s a c) -> p s a c", s=4, a=2)
                self.cp("act", vat4[:, 0:ns, 0:3:2, :], pv4[:, 0:ns, :, :], [self.Bk(3)], [Bf("vat")])
                self.load(U["va"][:, tA // 128:tA // 128 + ns, :], self.vat[:, 0:ns, :], [Bf("vat")], [Bf("va_" + u)],
                          group="st")
            if doB:
                for j in range(4):
                    bk = 6 + (j % 2)
                    for kc in range(8):
                        self.mm(self.bank(bk)[:, 0:NT], self.wk[:, kc, 256 + j * 128:256 + (j + 1) * 128],
                                hT_c[:, kc, 0:NT], kc == 0, kc == 7, [Bf("wk"), Bf(hTn_c)], [self.Bk(bk)])
                    self.cp("act", self.kbt[:, j, 0:NT], self.bank(bk)[:, 0:NT], [self.Bk(bk)], [Bf("kbt")])
                self.load(U["kbT"][:, :, tB:tB + NT], self.kbt[:, :, 0:NT], [Bf("kbt")], [Bf("kbT_" + u)], group="st")
                for s in range(ns):
                    bk = 2 + (s % 2) if not doA else 6 + (s % 2)
                    for kc in range(8):
                        self.mm(self.bank(bk)[:, :], hT_c[:, kc, s * 128:(s + 1) * 128], self.wk[:, kc, 768:1280],
                                kc == 0, kc == 7, [Bf("wk"), Bf(hTn_c)], [self.Bk(bk)])
                    dst = self.vbt[:, s, :].rearrange("p (j a c) -> p j a c", j=4, a=3)[:, :, 0:3:2, :]
                    srcp = self.bank(bk).rearrange("p (j a c) -> p j a c", j=4, a=2)
                    self.cp("dve", dst, srcp, [self.Bk(bk)], [Bf("vbt")])
                self.load(U["vb"][:, tB // 128:tB // 128 + ns, :], self.vbt[:, 0:ns, :], [Bf("vbt")], [Bf("vb_" + u)],
                          group="st")


        stageA(0)
        for tix in range(len(tiles)):
            if tix + 1 < len(tiles):
                stageA(tix + 1)
            stageB(tix)
            for _ in range(12):
                if self.pending_conv:
                    self.pending_conv.pop(0)()
        while self.pending_conv:
            self.pending_conv.pop(0)()
        if not self.resident_loaded:
            self.resident_loaded = True
            self.load(self.wpa_t[:, :, :], self.wpa_s, [Bf("wpa_s")], [Bf("wpa_t")], group="w")
            self.load(self.wpb_t[:, :, :], self.wpb_s, [Bf("wpb_s")], [Bf("wpb_t")], group="w")

    def mix_x_loads(self, U, ti):
        row0, nr = U["mix_tiles"][ti]
        NT = nr * 64
        tl0 = row0 * 64
        self.load_x(U["xL"][tl0:tl0 + NT, :], NT // 128)

    def mix_tile(self, u, U, ti, preloaded=False, prefetch_next=False, front_done=False, only_front=False):
        Bf = self.B
        samp = U["samp"]
        row0, nr = U["mix_tiles"][ti]
        NT = nr * 64
        ns = NT // 128
        qrow0 = U["qrow0"]
        tl0 = row0 * 64
        tq0 = (row0 - qrow0) * 64
        if not front_done:
            if not preloaded:
                self.load_x(U["xL"][tl0:tl0 + NT, :], ns)
            self.load(self.ropec[:, 0:NT], U["ropeQ"][0][:, tq0:tq0 + NT], (), [Bf("ropec")], group="x")
            self.load(self.ropes[:, 0:NT], U["ropeQ"][1][:, tq0:tq0 + NT], (), [Bf("ropes")], group="x")
            self.norm_transpose(ns, self.gpre, Bf("gpre"), self.hT, Bf("hT"))
        if only_front:
            return
        if ti >= 1 and self.stop is None and not samp:
            self.ffn_preload(u, U, ti - 1)
        wqa, wqab = self.next_wr()
        self.load(wqa[:, :, :], self.wqa_s, [Bf("wqa_s")], [wqab], group="w")
        wqb, wqbb = self.next_wr()
        self.load(wqb[:, :, :], self.wqb_s, [Bf("wqb_s")], [wqbb], group="w")
        def qb_group(j):
            bk = 6 + (j % 2)
            for kc in range(8):
                self.mm(self.bank(bk)[:, 0:NT], wqb[:, kc, j * 128:(j + 1) * 128], self.hT[:, kc, 0:NT], kc == 0, kc == 7,
                        [wqbb, Bf("hT")], [self.Bk(bk)])

        def qb_copy(j):
            bk = 6 + (j % 2)
            self.cp("act", self.qz[0][0:64, j, 0:NT], self.bank(bk)[0:64, 0:NT], [self.Bk(bk)], [Bf("qz")])
            self.cp("act", self.qz[1][64:128, j, 0:NT], self.bank(bk)[64:128, 0:NT], [self.Bk(bk)], [Bf("qz")])

        for j in range(4):
            for kc in range(8):
                self.mm(self.bank(j)[:, 0:NT], wqa[:, kc, j * 128:(j + 1) * 128], self.hT[:, kc, 0:NT], kc == 0, kc == 7,
                        [wqab, Bf("hT")], [self.Bk(j)])
        qb_group(0)
        qb_group(1)
        for j in range(4):
            self.qk_norm_rope(j, NT, 0, self.ropec[:, 0:NT], self.ropes[:, 0:NT],
                              [(slice(0, 64), self.qaz[0][0:64, j, 0:NT]), (slice(64, 128), self.qaz[1][64:128, j, 0:NT])],
                              Bf("qaz"), [Bf("ropec"), Bf("ropes")])
            if j == 0:
                qb_copy(0)
                qb_copy(1)
                qb_group(2)
                qb_group(3)
            if j == 1:
                qb_copy(2)
                qb_copy(3)
        self.trickle_late(4)
        if self.stop == "mix_q":
            return
        SA = U["SA"]
        nblk = SA // 1024
        for g in range(2):
            vcol = slice(0, 128) if g == 0 else slice(64, 192)
            pend = None
            for blk in range(nblk):
                ri = self.kv_i % 2
                self.kv_i += 1
                kr, vr = self.kring[ri], self.vring[ri]
                krb, vrb = Bf("kring%d" % ri), Bf("vring%d" % ri)
                self.load(kr[:, :], U["kaT"][:, blk * 1024:(blk + 1) * 1024], [Bf("kaT_" + u)], [krb], group="kv")
                self.load(vr[:, :, :], U["va"][:, blk * 8:(blk + 1) * 8, vcol], [Bf("va_" + u)], [vrb], group="kv")
                for c in range(8):
                    for j in range(4):
                        self.mm(self.bank(4 + j)[:, 0:NT], kr[:, c * 128:(c + 1) * 128], self.qaz[g][:, j, 0:NT], True, True,
                                [krb, Bf("qaz")], [self.Bk(4 + j)])
                    pts = []
                    for hp in range(2):
                        pt, ptb = self.next_pt()
                        src = self.PS[2 + hp][:, :].rearrange("p (b n) -> p b n", b=2)[:, :, 0:NT]
                        self.act(pt[:, :, 0:NT], src, AF.Exp, [self.Bk(4 + 2 * hp), self.Bk(5 + 2 * hp)], [ptb], scale=0.125)
                        pts.append((pt, ptb))
                    first = (blk == 0 and c == 0)
                    last = (blk == nblk - 1 and c == 7)
                    cur = (pts, vr, vrb, c, first, last)
                    if pend is not None:
                        self._gqa_pv(pend, NT)
                    pend = cur
            self._gqa_pv(pend, NT)
            os_ = slice(0, 64) if g == 0 else slice(64, 128)
            ds_ = slice(64, 128) if g == 0 else slice(0, 64)
            for j in range(4):
                rc, rcb = self.next_tmp()
                self.act_recip(rc[ds_, 0:NT], self.bank(j)[ds_, 0:NT], [self.Bk(j)], [rcb])
                self.tt("dve", self.oaT[os_, j, 0:NT], self.bank(j)[os_, 0:NT], rc[ds_, 0:NT], ALU.mult,
                        [self.Bk(j), rcb], [Bf("oaT")])
        self.trickle_late(4)
        if self.stop == "mix_gqa":
            return
        rows = U["rows"]
        qrows = [rows[(row0 - qrow0) + i] for i in range(nr)]
        cmin = min(ci for lst in qrows for (ci, m, mc) in lst)
        cmax = max(ci for lst in qrows for (ci, m, mc) in lst)
        nb = cmax - cmin + 1
        assert nb <= 9
        self.load(self.kbb[:, :, 0:nb * 128], U["kbT"][:, :, cmin * 128:(cmax + 1) * 128], [Bf("kbT_" + u)],
                  [Bf("kbb"), Bf("gT")], group="kv")
        self.load(self.vbb[:, 0:nb, :], U["vb"][:, cmin:cmax + 1, :], [Bf("vb_" + u)], [Bf("vbb"), Bf("gT")], group="kv")
        if self.stop == "mix_nal":
            return
        steps = []
        for i in range(nr):
            lst = qrows[i]
            for n_, (ci, m, mc) in enumerate(lst):
                steps.append((i, ci - cmin, m, mc, n_ == 0, n_ == len(lst) - 1))
        DEPTH = 3
        queue = []

        def na_finish(item):
            (i, lc, first, last, pt, ptb) = item
            ob = i % 2
            po = self.bank(ob).rearrange("p (h q) -> p h q", h=8)
            self._na_pv((pt, ptb, lc, first, last), po, ob)
            if not last:
                return
            qc = slice(i * 64, (i + 1) * 64)
            rc, rcb = self.next_tmp()
            rc3 = rc[:, :].rearrange("p (h q) -> p h q", h=8)
            self.act(rc3[64:128, 0:8:2, :], po[64:128, 0:8:2, :], AF.Ln, [self.Bk(ob)], [rcb])
            self.act(rc3[0:64, 1:8:2, :], po[0:64, 1:8:2, :], AF.Ln, [self.Bk(ob), rcb], [rcb])
            self.act(rc3[64:128, 0:8:2, :], rc3[64:128, 0:8:2, :], AF.Exp, [rcb], [rcb], scale=-1.0)
            self.act(rc3[0:64, 1:8:2, :], rc3[0:64, 1:8:2, :], AF.Exp, [rcb], [rcb], scale=-1.0)
            self.tt("dve", self.obT[0:64, :, qc], po[0:64, 0:8:2, :], rc3[64:128, 0:8:2, :], ALU.mult,
                    [self.Bk(ob), rcb], [Bf("obT")])
            self.tt("dve", self.obT[64:128, :, qc], po[64:128, 1:8:2, :], rc3[0:64, 1:8:2, :], ALU.mult,
                    [self.Bk(ob), rcb], [Bf("obT")])

        for t, (i, lc, m, mc, first, last) in enumerate(steps):
            qc = slice(i * 64, (i + 1) * 64)
            sp_ = 1 + (t % 3)
            pbufs = [self.Bk(2 * sp_), self.Bk(2 * sp_ + 1)]
            ps8 = self.PS[sp_][:, :].rearrange("p (h q) -> p h q", h=8)[:, :, 0:64]
            for a in range(2):
                for j in range(4):
                    self.mm(ps8[:, a * 4 + j, :], self.kbb[:, j, lc * 128:(lc + 1) * 128], self.qz[a][:, j, qc], True, True,
                            [Bf("kbb"), Bf("qz")], [self.Bk(2 * sp_ + a)])
            sc, scb = self.next_tmp()
            sc3 = sc[:, :].rearrange("p (h q) -> p h q", h=8)
            self.stt("dve", sc3, ps8, 0.125, self.wt[:, :, m, :], ALU.mult, ALU.add, pbufs + [Bf("wt")], [scb])
            pt, ptb = self.next_pt()
            self.act(pt[:, 0, :], sc[:, :], AF.Exp, [scb, Bf(U["maskbuf"])], [ptb], bias=U["mask"][:, mc:mc + 1])
            queue.append((i, lc, first, last, pt, ptb))
            if len(queue) > DEPTH:
                na_finish(queue.pop(0))
        while queue:
            na_finish(queue.pop(0))
        if self.debug and ti == 0:
            for nm, t in (("oaT", self.oaT), ("obT", self.obT)):
                dd = self.nc.dram_tensor("dbg_%s_%s" % (nm, u), [128, 4, 512], BF16, kind="ExternalOutput").ap()
                self.load(dd, t[:, :, :], [Bf(nm if not nm.startswith("qz") else "qz")], [Bf("dbg_" + nm)], group="st")
        self.trickle_late(4)
        if self.stop == "mix_na":
            return
        wpa4, wpab = self.wpa_t, Bf("wpa_t")
        wpb4, wpbb = self.wpb_t, Bf("wpb_t")
        for pn in range(4):
            wg, wgb = self.next_wr()
            self.load(wg[:, :, :], self.wg_s[pn], [Bf("wg_s")], [wgb], group="w")
            for f in range(2):
                fc = pn * 2 + f
                b0 = 4 * (fc % 2)
                for kc in range(8):
                    self.mm(self.bank(b0)[:, 0:NT], wg[:, kc, f * 128:(f + 1) * 128], self.hT[:, kc, 0:NT], kc == 0, kc == 7,
                            [wgb, Bf("hT")], [self.Bk(b0)])
                for kc in range(8):
                    self.mm(self.bank(b0 + 1)[:, 0:NT], wg[:, kc, 256 + f * 128:256 + (f + 1) * 128], self.hT[:, kc, 0:NT],
                            kc == 0, kc == 7, [wgb, Bf("hT")], [self.Bk(b0 + 1)])
                for kc in range(4):
                    self.mm(self.bank(b0 + 2)[:, 0:NT], wpa4[:, kc, fc * 128:(fc + 1) * 128], self.oaT[:, kc, 0:NT],
                            kc == 0, kc == 3, [wpab, Bf("oaT")], [self.Bk(b0 + 2)])
                for kc in range(4):
                    self.mm(self.bank(b0 + 3)[:, 0:NT], wpb4[:, kc, fc * 128:(fc + 1) * 128], self.obT[:, kc, 0:NT],
                            kc == 0, kc == 3, [wpbb, Bf("obT")], [self.Bk(b0 + 3)])
                sa, sab = self.next_tmp()
                self.act(sa[:, 0:NT], self.bank(b0)[:, 0:NT], AF.Sigmoid, [self.Bk(b0), Bf("bg")], [sab],
                         bias=self.bg[:, fc:fc + 1])
                sg, sgb = self.next_tmp()
                self.act(sg[:, 0:NT], self.bank(b0 + 1)[:, 0:NT], AF.Sigmoid, [self.Bk(b0 + 1), Bf("bg")], [sgb],
                         bias=self.bg[:, 8 + fc:9 + fc])
                self.tt("dve", sa[:, 0:NT], self.bank(b0 + 2)[:, 0:NT], sa[:, 0:NT], ALU.mult, [self.Bk(b0 + 2), sab], [sab])
                self.tt("dve", sg[:, 0:NT], self.bank(b0 + 3)[:, 0:NT], sg[:, 0:NT], ALU.mult, [self.Bk(b0 + 3), sgb], [sgb])
                self.tt("pool", self.mT[:, fc, 0:NT], sa[:, 0:NT], sg[:, 0:NT], ALU.add, [sab, sgb], [Bf("mT")])
        if self.debug and ti == 0:
            dd = self.nc.dram_tensor("dbg_mT_%s" % u, [128, 8, 512], BF16, kind="ExternalOutput").ap()
            self.load(dd, self.mT[:, :, :], [Bf("mT")], [Bf("dbg_mT")], group="st")
        self.trickle_late(4)
        if self.stop == "mix_merge":
            return
        wo = []
        for hf in range(2):
            w_, wb_ = self.next_wr()
            self.load(w_[:, :, :], self.wo_s[hf], [Bf("wo_s")], [wb_], group="w")
            wo.append((w_, wb_))
        self.memset("pool", self.ss[:, 0:ns], 0.0, [Bf("ss")])
        for s in range(ns):
            pp = s
            for hf in range(2):
                bk = pp * 2 + hf
                for kc in range(8):
                    self.mm(self.bank(bk)[:, :], self.mT[:, kc, s * 128:(s + 1) * 128], wo[hf][0][:, kc, :], kc == 0, kc == 7,
                            [Bf("mT"), wo[hf][1]], [self.Bk(bk)])
        for s in range(ns):
            pp = s
            mixp = self.PS[pp][:, :]
            pbufs = [self.Bk(pp * 2), self.Bk(pp * 2 + 1)]
            xn = self.xn[s % 2]
            self.act(xn[:, :], mixp, AF.Square, pbufs + [Bf("ss")], [Bf("xn%d" % (s % 2)), Bf("ss")],
                     accum=self.ss[:, s:s + 1])
            rsb = Bf("rstd_w%d" % s)
            self.act_rsqrt(self.rstd[:, s:s + 1], self.ss[:, s:s + 1], 1.0 / D, [Bf("ss")], [rsb])
            t0, t0b = self.next_tmp()
            t1, t1b = self.next_tmp()
            for hf, (tq, tqb) in enumerate(((t0, t0b), (t1, t1b))):
                self.stt("dve", tq[:, :], self.bank(pp * 2 + hf)[:, :], self.rstd[:, s:s + 1],
                         self.gpm[:, hf * 512:(hf + 1) * 512], ALU.mult, ALU.mult,
                         [self.Bk(pp * 2 + hf), rsb, Bf("gpm")], [tqb])
                self.tt("dve", self.xt[:, s, hf * 512:(hf + 1) * 512], self.xt[:, s, hf * 512:(hf + 1) * 512], tq[:, :],
                        ALU.add, [Bf("xt"), tqb], [Bf("xt")])
        if self.stop == "mix_wout":
            return
        if ti >= 1 and self.stop is None:
            self.trickle_late(1000)
            self.ffn_pref = []
            for pn in range(2):
                wu_, wub_ = self.next_wr()
                self.load(wu_[:, :, :], self.wup_s[pn], [Bf("wup_s")], [wub_], group="w")
                self.ffn_pref.append((wu_, wub_))
        own0 = U["own0"]
        for s in range(ns):
            ta = tl0 + s * 128
            lo = max(ta, own0 * 64)
            hi = min(ta + 128, own0 * 64 + 2048)
            if hi <= lo:
                continue
            yt = sorted(set([(lo - own0 * 64) // 512, (hi - 1 - own0 * 64) // 512]))
            self.load(U["y"][lo - own0 * 64:hi - own0 * 64, :], self.xt[lo - ta:hi - ta, s, :], [Bf("xt")],
                      [Bf("y_%s_%d" % (u, k)) for k in yt], group="st")
        self.norm_transpose(ns, self.gpf, Bf("gpf"), self.hT, Bf("hT"))
        if prefetch_next:
            self.mix_x_loads(U, ti + 1)
        if ti >= 1 and self.stop is None and not samp:
            fi = ti - 1
            cb = (U["own0"] - U["qrow0"]) * 64 + fi * 512
            k = cb + 512 - tq0
            assert 0 <= k < NT
            self.cp("pool", self.hw[:, :, 513:514], self.hT[:, :, k:k + 1], [Bf("hT")], [Bf("hw")])
        self.load(U["h2T"][:, :, 1 + tq0:1 + tq0 + NT], self.hT[:, :, 0:NT], [Bf("hT")], [Bf("h2T_" + u)], group="st")

    def _gqa_pv(self, pend, NT):
        pts, vr, vrb, c, first, last = pend
        for j in range(4):
            pt, ptb = pts[j // 2]
            self.mm(self.bank(j)[:, 0:NT], vr[:, c, :], pt[:, j % 2, 0:NT], first, last, [vrb, ptb], [self.Bk(j)])

    def _na_pv(self, pend, po, ob):
        pt, ptb, lc, first, last = pend
        pt3 = pt[:, 0, :].rearrange("p (h q) -> p h q", h=8)
        for h in range(8):
            j, a = h // 2, h % 2
            c0 = j * 192 + a * 64
            self.mm(po[:, h, :], self.vbb[:, lc, c0:c0 + 128], pt3[:, a * 4 + j, :], first and h == 0, last,
                    [self.B("vbb"), ptb], [self.Bk(ob)], skip=True)

    def ffn_preload(self, u, U, fi):
        Bf = self.B
        cb = (U["own0"] - U["qrow0"]) * 64 + fi * 512
        self.load(self.hw[:, :, 0:513], U["h2T"][:, :, cb:cb + 513], [Bf("h2T_" + u)], [Bf("hw")], group="x")

    def ffn_tile(self, u, U, fi, preloaded=False, between=None):
        Bf = self.B
        samp = U["samp"]
        cb = (U["own0"] - U["qrow0"]) * 64 + fi * 512
        self.trickle_late(1000)
        if not preloaded:
            self.load(self.hw[:, :, :], U["h2T"][:, :, cb:cb + 514], [Bf("h2T_" + u)], [Bf("hw")], group="x")
        if samp and fi == 0:
            self.ts("pool", self.hw[:, :, 0:1], self.hw[:, :, 0:1], self.flags[:, 0:1], None, ALU.mult, None,
                    [Bf("hw"), Bf("flags")], [Bf("hw")])
        if samp and fi == 3 and not preloaded:
            self.ts("pool", self.hw[:, :, 513:514], self.hw[:, :, 513:514], self.flags[:, 1:2], None, ALU.mult, None,
                    [Bf("hw"), Bf("flags")], [Bf("hw")])
        hwh = self.hw[:, :, 0:514:513]
        for pn in range(11):
            if pn < 2 and getattr(self, "ffn_pref", None):
                wu, wub = self.ffn_pref.pop(0)
            else:
                wu, wub = self.next_wr()
                self.load(wu[:, :, :], self.wup_s[pn], [Bf("wup_s")], [wub], group="w")
            for f in range(2):
                fc = pn * 2 + f
                hb = 6 + (fc % 2)
                cw = self.conv
                info = []
                for half in range(2):
                    cid = fc + 22 * half
                    bk = (fc % 3) * 2 + half
                    wc = slice(half * 256 + f * 128, half * 256 + (f + 1) * 128)
                    for kc in range(8):
                        self.mm(self.bank(bk)[:, :], wu[:, kc, wc], self.hw[:, kc, 1:513], kc == 0, kc == 7,
                                [wub, Bf("hw")], [self.Bk(bk)])
                    uh = self.bank(hb)[:, half * 2:half * 2 + 2]
                    for kc in range(8):
                        self.mm(uh, wu[:, kc, wc], hwh[:, kc, :], kc == 0, kc == 7, [wub, Bf("hw")], [self.Bk(hb)])
                    c_, cb_ = self.next_tmp()
                    info.append((cid, bk, uh, c_, cb_))
                for (cid, bk, uh, c_, cb_) in info:
                    self.act(c_[:, :], self.bank(bk)[:, :], AF.Identity, [self.Bk(bk), Bf("conv")], [cb_],
                             bias=cw[:, cid, 3:4], scale=cw[:, cid, 1:2])
                for (cid, bk, uh, c_, cb_) in info:
                    self.stt("dve", c_[:, 1:512], self.bank(bk)[:, 0:511], cw[:, cid, 0:1], c_[:, 1:512], ALU.mult, ALU.add,
                             [self.Bk(bk), cb_, Bf("conv")], [cb_])
                for (cid, bk, uh, c_, cb_) in info:
                    self.stt("dve", c_[:, 0:511], self.bank(bk)[:, 1:512], cw[:, cid, 2:3], c_[:, 0:511], ALU.mult, ALU.add,
                             [self.Bk(bk), cb_, Bf("conv")], [cb_])
                for (cid, bk, uh, c_, cb_) in info:
                    self.stt("dve", c_[:, 0:1], uh[:, 0:1], cw[:, cid, 0:1], c_[:, 0:1], ALU.mult, ALU.add,
                             [self.Bk(hb), cb_, Bf("conv")], [cb_])
                for (cid, bk, uh, c_, cb_) in info:
                    self.stt("dve", c_[:, 511:512], uh[:, 1:2], cw[:, cid, 2:3], c_[:, 511:512], ALU.mult, ALU.add,
                             [self.Bk(hb), cb_, Bf("conv")], [cb_])
                (_, _, _, cg, cgb), (_, _, _, cv, cvb) = info
                self.act(cg[:, :], cg[:, :], AF.Gelu_apprx_tanh, [cgb], [cgb])
                self.tt("pool", self.gT[:, fc, :], cg[:, :], cv[:, :], ALU.mult, [cgb, cvb],
                        [Bf("gT"), Bf("kbb"), Bf("vbb")])
        self.memset("pool", self.ss[:, 0:4], 0.0, [Bf("ss")])
        dn_banks = ([2, 3, 4, 5], [6, 7, 2, 3])
        for ps_ in range(2):
            for q6 in range(6):
                k0 = q6 * 4
                nk = min(4, 22 - k0)
                wd, wdb = self.next_wr()
                wd2 = wd[:, :, :].rearrange("p a b -> p (a b)").rearrange("p (k n) -> p k n", k=4)
                self.load(wd2[:, 0:nk, :], self.wdn_s[:, k0:k0 + nk, :], [Bf("wdn_s")], [wdb], group="w")
                for si in range(2):
                    s = ps_ * 2 + si
                    for hf in range(2):
                        bk = dn_banks[ps_][si * 2 + hf]
                        for kk in range(nk):
                            fc = k0 + kk
                            self.mm(self.bank(bk)[:, :], self.gT[:, fc, s * 128:(s + 1) * 128],
                                    wd2[:, kk, hf * 512:(hf + 1) * 512], fc == 0, fc == 21,
                                    [Bf("gT"), wdb], [self.Bk(bk)])
            if ps_ == 0 and between is not None:
                between()
            for si in range(2):
                s = ps_ * 2 + si
                pi = dn_banks[ps_][si * 2] // 2
                fp = self.PS[pi][:, :]
                pbufs = [self.Bk(pi * 2), self.Bk(pi * 2 + 1)]
                xn = self.xn[s % 2]
                self.act(xn[:, :], fp, AF.Square, pbufs + [Bf("ss")], [Bf("xn%d" % (s % 2)), Bf("ss")],
                         accum=self.ss[:, s:s + 1])
                self.act_rsqrt(self.rstd[:, s:s + 1], self.ss[:, s:s + 1], 1.0 / D, [Bf("ss")], [Bf("rstd")])
                fo, fob = self.fo[s % 2], Bf("fo%d" % (s % 2))
                self.stt("dve", fo[:, :], fp, self.rstd[:, s:s + 1], self.gpo[:, :], ALU.mult, ALU.mult,
                         pbufs + [Bf("rstd"), Bf("gpo")], [fob])
                ybuf = Bf("y_%s_%d" % (u, fi))
                self.accum_store(U["y"][fi * 512 + s * 128:fi * 512 + (s + 1) * 128, :], fo[:, :], [fob, ybuf], [ybuf])


_CACHE = {}


def host_consts(inp):
    f32 = np.float32
    c = {}
    c["c_gpre"] = np.ascontiguousarray(inp["pre_mix_norm"].reshape(8, 128).T).astype(f32)
    c["c_gpf"] = np.ascontiguousarray(inp["pre_ffn_norm"].reshape(8, 128).T).astype(f32)
    c["c_gpm"] = np.ascontiguousarray(np.broadcast_to(inp["post_mix_norm"].reshape(1, D), (128, D))).astype(f32)
    c["c_gpo"] = np.ascontiguousarray(np.broadcast_to(inp["post_ffn_norm"].reshape(1, D), (128, D))).astype(f32)
    c["c_bg"] = np.ascontiguousarray(inp["b_gate"].reshape(16, 128).T).astype(f32)
    qw = np.tile(inp["q_norm"].reshape(64), 2)
    kw = np.tile(inp["k_norm"].reshape(64), 2)
    c["c_qkw"] = np.ascontiguousarray(np.stack([qw, kw], axis=1)).astype(f32)
    cw = inp["conv_w"].reshape(3, 2 * DFF)
    cb = inp["conv_b"].reshape(1, 2 * DFF)
    c4 = np.concatenate([cw, cb], axis=0)
    c["c_conv"] = np.ascontiguousarray(c4.reshape(4, 44, 128).transpose(2, 1, 0).reshape(128, 44 * 4)).astype(f32)
    rpb = inp["rpb"].reshape(8, 15, 31)
    a = (np.arange(128) // 64)[:, None, None, None]
    ck = (np.arange(128) % 64)[:, None, None, None]
    hp = np.arange(8)
    h = (2 * (hp % 4) + hp // 4)[None, :, None, None]
    m = np.arange(15)[None, None, :, None]
    cq = np.arange(64)[None, None, None, :]
    dr = np.minimum(m + a, 14)
    dc = np.clip(ck - cq, -15, 15) + 15
    c["c_wt"] = np.ascontiguousarray(rpb[h, dr, dc].reshape(128, 8 * 15 * 64)).astype(f32)
    return c


def static_consts():
    f32 = np.float32
    c = {}
    ckk = np.arange(64)[:, None]
    cqq = np.arange(64)[None, :]
    cs = np.clip(cqq - 8, 0, 48)
    inwin = (ckk >= cs) & (ckk < cs + 16)
    cm = np.where(inwin, 0.0, NEG).astype(f32)
    c["c_cm"] = np.concatenate([cm, cm], axis=0)
    c["c_mp"] = PROMPT_MASK.astype(f32)
    c["c_ident"] = np.eye(128).astype(ml_dtypes.bfloat16)
    bo = np.zeros((128, 128), f32)
    bo[0:64, 0:64] = 1
    bo[64:128, 64:128] = 1
    c["c_bones"] = bo
    lt = np.zeros((128, 128), f32)
    for d in range(128):
        if d % 32 < 16:
            lt[d + 16, d] = -1.0
        else:
            lt[d - 16, d] = 1.0
    c["c_rotm"] = lt
    ck_, sk_ = rope_tables(8192)
    c["c_cosk"] = ck_
    c["c_sink"] = sk_
    return c


def get_builder(units, debug, stop=None):
    key = (tuple(units), debug, stop)
    if key not in _CACHE:
        _CACHE[key] = Builder(units=units, debug=debug, stop=stop)
    return _CACHE[key]


def make_in_maps(inputs):
    inp = {k: np.asarray(v) for k, v in inputs.items()}
    xpr = inp["x_prompt"]
    xsa = inp["x_sample"]
    hc = host_consts(inp)
    sc = static_consts()
    wts = dict(
        w_in=np.ascontiguousarray(inp["w_in"].reshape(D, DIN)),
        w_pa=np.ascontiguousarray(inp["w_proj_a"].reshape(512, D)),
        w_pb=np.ascontiguousarray(inp["w_proj_b"].reshape(512, D)),
        w_out=np.ascontiguousarray(inp["w_out"].reshape(D, D)),
        w_up=np.ascontiguousarray(inp["w_up"].reshape(D, 2 * DFF)),
        w_dn=np.ascontiguousarray(inp["w_down"].reshape(DFF, D)),
    )
    cosk, sink = sc["c_cosk"], sc["c_sink"]
    in_maps = []
    for c in range(NCORES):
        sq, cq = c // 4, c % 4
        m = {}
        m["xp"] = np.ascontiguousarray(xpr[4 * c:4 * c + 4])
        m["xsf"] = np.ascontiguousarray(xsa[sq])
        r0 = 32 * cq
        loc = np.zeros((2816, D), np.float32)
        g0 = (r0 - 6) * 64
        lo, hi = max(g0, 0), min(g0 + 2816, 8192)
        loc[lo - g0:hi - g0] = xsa[sq, lo:hi]
        m["xsl"] = loc
        tq = (r0 - 1) * 64 + np.arange(2176)
        tq = np.clip(tq, 0, 8191)
        m["c_cosq"] = np.ascontiguousarray(cosk[:, tq])
        m["c_sinq"] = np.ascontiguousarray(sink[:, tq])
        m["c_ms"] = SAMPLE_MASKS[cq].astype(np.float32)
        fl = np.ones((128, 2), np.float32)
        if cq == 0:
            fl[:, 0] = 0.0
        if cq == 3:
            fl[:, 1] = 0.0
        m["c_flags"] = fl
        m.update(hc)
        m.update(sc)
        m.update(wts)
        in_maps.append(m)
    return in_maps


def run(inputs, units=("p0", "p1", "p2", "p3", "s"), debug=False, stop=None):
    b = get_builder(units, debug, stop)
    in_maps = make_in_maps(inputs)
    res = run_bass_kernel_spmd(b.nc, in_maps, core_ids=list(range(NCORES)))
    return res


def kernel(**inputs):
    res = run(inputs)
    yp = np.concatenate([np.asarray(r["yp"]) for r in res.results], axis=0).astype(np.float32)
    ys = np.stack([np.concatenate([np.asarray(res.results[s * 4 + q]["ys"]) for q in range(4)], axis=0)
                   for s in range(2)], axis=0).astype(np.float32)
    return (yp, ys)
```
